# Optimizing a Trainium2 kernel written in Bass

```python
import math
import jax, jax.numpy as jnp
from jax import lax
import numpy as np

D_MODEL = 1024
BATCH = 8
SEQ = 4096
DEPTH = 1

MLA_HEADS = 8
QK_NOPE = 64
QK_ROPE = 32
V_HEAD = 64
Q_LORA = 256
KV_LORA = 128
ROPE_THETA = 10000.0
Q_BLOCK = 128
MLA_WIDTH = MLA_HEADS * V_HEAD
RWKV_HEADS = 8
RWKV_HEAD = 64
RWKV_WIDTH = RWKV_HEADS * RWKV_HEAD
DECAY_LORA = 64
A_LORA = 64
GATE_LORA = 128
GN_EPS = 64e-5
N_BRANCHES = 2
MIX_WIDTH = MLA_WIDTH + RWKV_WIDTH
FFN_HIDDEN = 2816
CONV_WIDTH = 3
NORM_EPS = 1e-6
MLA_COLS = Q_LORA + KV_LORA + QK_ROPE
RWKV_COLS = 3 * RWKV_WIDTH + A_LORA + 2 * DECAY_LORA + 2 * GATE_LORA
GATE_COLS = N_BRANCHES * D_MODEL
IN_COLS = MLA_COLS + RWKV_COLS + GATE_COLS

kernel_name = "hybrid_mla_rwkv7_gated_encoder"


def _split(t, sizes):
    out, start = [], 0
    for s in sizes:
        out.append(t[..., start:start + s])
        start += s
    return out


def rms_norm(t, g, eps=NORM_EPS):
    tf = t.astype(jnp.float32)
    y = tf * lax.rsqrt(jnp.mean(tf * tf, axis=-1, keepdims=True) + eps)
    return (y * g.astype(jnp.float32)).astype(t.dtype)


def rope(t, cos, sin):
    half = t.shape[-1] // 2
    t1, t2 = t[..., :half], t[..., half:]
    return jnp.concatenate([t1 * cos - t2 * sin, t2 * cos + t1 * sin], axis=-1)


def centered_shift(u, mu_prev, mu_next):
    zero = jnp.zeros_like(u[:, :1])
    prev = jnp.concatenate([zero, u[:, :-1]], axis=1)
    nxt = jnp.concatenate([u[:, 1:], zero], axis=1)
    return u + mu_prev * (prev - u) + mu_next * (nxt - u)


def dwconv_centered(u, w, b):
    up = jnp.pad(u, ((0, 0), (1, 1), (0, 0)))
    return up[:, :-2] * w[0] + up[:, 1:-1] * w[1] + up[:, 2:] * w[2] + b


def blocked_mla_attention(q_nope, q_rope, k_nope, k_rope, v):
    B, S, H, _ = q_nope.shape
    nb = S // Q_BLOCK
    scale = 1.0 / math.sqrt(QK_NOPE + QK_ROPE)
    qn = jnp.moveaxis(q_nope.reshape(B, nb, Q_BLOCK, H, QK_NOPE), 1, 0)
    qr = jnp.moveaxis(q_rope.reshape(B, nb, Q_BLOCK, H, QK_ROPE), 1, 0)

    def one_block(args):
        qn_b, qr_b = args
        s = (jnp.einsum('bqhd,bkhd->bhqk', qn_b, k_nope)
             + jnp.einsum('bqhd,bkd->bhqk', qr_b, k_rope))
        p = jax.nn.softmax(s.astype(jnp.float32) * scale, axis=-1)
        return jnp.einsum('bhqk,bkhd->bqhd', p.astype(v.dtype), v)

    o = lax.map(one_block, (qn, qr))
    return jnp.moveaxis(o, 0, 1).reshape(B, S, H * V_HEAD)


def mla_branch(cols, cos, sin, q_a_norm_g, kv_a_norm_g, w_uq, w_ukv, qn_g, qr_g, kn_g, kr_g):
    B, S, _ = cols.shape
    c_q, c_kv, k_rope = _split(cols, [Q_LORA, KV_LORA, QK_ROPE])
    c_q = rms_norm(c_q, q_a_norm_g)
    c_kv = rms_norm(c_kv, kv_a_norm_g)
    q = (c_q @ w_uq).reshape(B, S, MLA_HEADS, QK_NOPE + QK_ROPE)
    kv = (c_kv @ w_ukv).reshape(B, S, MLA_HEADS, QK_NOPE + V_HEAD)
    q_nope, q_rope = q[..., :QK_NOPE], q[..., QK_NOPE:]
    k_nope, v = kv[..., :QK_NOPE], kv[..., QK_NOPE:]
    q_nope = rms_norm(q_nope, qn_g)
    k_nope = rms_norm(k_nope, kn_g)
    q_rope = rope(rms_norm(q_rope, qr_g), cos[:, :, None, :], sin[:, :, None, :])
    k_rope = rope(rms_norm(k_rope, kr_g), cos, sin)
    return blocked_mla_attention(q_nope, q_rope, k_nope, k_rope, v)


def wkv7_scan(r, w, k, v, a, b, reverse):
    B, S, H, N = r.shape
    xs = tuple(jnp.moveaxis(t.astype(jnp.float32), 1, 0) for t in (r, w, k, v, a, b))

    def step(state, inp):
        r_t, w_t, k_t, v_t, a_t, b_t = inp
        sa = jnp.einsum('bhvk,bhk->bhv', state, a_t)
        state = (state * w_t[:, :, None, :] + sa[..., None] * b_t[:, :, None, :]
                 + v_t[..., None] * k_t[:, :, None, :])
        return state, jnp.einsum('bhvk,bhk->bhv', state, r_t)

    s0 = jnp.zeros((B, H, N, N), jnp.float32)
    _, out = lax.scan(step, s0, xs, reverse=reverse)
    return jnp.moveaxis(out, 0, 1)


def head_group_norm(o, g, b):
    mu = jnp.mean(o, axis=-1, keepdims=True)
    var = jnp.mean(jnp.square(o - mu), axis=-1, keepdims=True)
    y = (o - mu) * lax.rsqrt(var + GN_EPS)
    return (y * g.astype(jnp.float32).reshape(RWKV_HEADS, RWKV_HEAD)
            + b.astype(jnp.float32).reshape(RWKV_HEADS, RWKV_HEAD))


def rwkv7_branch(cols, shift_mu, w0, w2, a0, a2, g2, k_k, k_a, r_k, ln_g, ln_b):
    B, S, _ = cols.shape
    u = centered_shift(cols, shift_mu[0], shift_mu[1])
    r, k, v, a_lo, dlo_f, dlo_b, glo_f, glo_b = _split(
        u, [RWKV_WIDTH, RWKV_WIDTH, RWKV_WIDTH, A_LORA, DECAY_LORA, DECAY_LORA, GATE_LORA, GATE_LORA])
    heads = lambda t: t.reshape(B, S, RWKV_HEADS, RWKV_HEAD)
    a = jax.nn.sigmoid(a0 + a_lo @ a2)
    kkf = heads(k * k_k).astype(jnp.float32)
    kk = kkf / jnp.maximum(jnp.sqrt(jnp.sum(kkf * kkf, axis=-1, keepdims=True)), 1e-12)
    k = k * (1.0 + (a - 1.0) * k_a)
    rh, kh, vh, ah = heads(r), heads(k), heads(v), heads(a)
    bonus = (jnp.sum(rh * kh * r_k, axis=-1, keepdims=True) * vh).astype(jnp.float32)
    b_vec = kk * ah.astype(jnp.float32)

    def direction(dlo, glo, w0_d, w2_d, g2_d, reverse):
        w = -jax.nn.softplus(-(w0_d + jnp.tanh(dlo) @ w2_d)) - 0.5
        decay = jnp.exp(-jnp.exp(w.astype(jnp.float32)))
        o = wkv7_scan(rh, heads(decay), kh, vh, -kk, b_vec, reverse)
        o = (head_group_norm(o, ln_g, ln_b) + bonus).reshape(B, S, RWKV_WIDTH)
        g = jax.nn.sigmoid(glo) @ g2_d
        return o.astype(cols.dtype) * g

    fwd = direction(dlo_f, glo_f, w0[0], w2[0], g2[0], False)
    bwd = direction(dlo_b, glo_b, w0[1], w2[1], g2[1], True)
    return fwd + bwd


def setup_inputs(seed: int = 0) -> dict:
    key = jax.random.key(seed)
    ks = iter(jax.random.split(key, 40))
    nrm = lambda shape, s: jax.random.normal(next(ks), shape, jnp.float32) * s
    gain = lambda shape: 1.0 + nrm(shape, 0.05)
    L, D = DEPTH, D_MODEL
    x = jax.random.normal(next(ks), (BATCH, SEQ, D), jnp.float32)
    offs = jax.random.randint(next(ks), (BATCH, 1), 0, 2048, dtype=jnp.int32)
    positions = offs + jnp.arange(SEQ, dtype=jnp.int32)[None, :]
    return {
        "x": x,
        "positions": positions,
        "norm_mix_g": gain((L, D)),
        "w_in": nrm((L, D, IN_COLS), D ** -0.5),
        "b_gate": nrm((L, N_BRANCHES, D), 0.1),
        "q_a_norm_g": gain((L, Q_LORA)),
        "kv_a_norm_g": gain((L, KV_LORA)),
        "w_uq": nrm((L, Q_LORA, MLA_HEADS * (QK_NOPE + QK_ROPE)), Q_LORA ** -0.5),
        "w_ukv": nrm((L, KV_LORA, MLA_HEADS * (QK_NOPE + V_HEAD)), KV_LORA ** -0.5),
        "qn_norm_g": gain((L, QK_NOPE)),
        "qr_norm_g": gain((L, QK_ROPE)),
        "kn_norm_g": gain((L, QK_NOPE)),
        "kr_norm_g": gain((L, QK_ROPE)),
        "shift_mu": jax.random.uniform(next(ks), (L, 2, RWKV_COLS), jnp.float32, 0.0, 0.5),
        "w0": jax.random.uniform(next(ks), (L, 2, RWKV_WIDTH), jnp.float32, -6.0, -1.0),
        "w2": nrm((L, 2, DECAY_LORA, RWKV_WIDTH), 0.1 * DECAY_LORA ** -0.5),
        "a0": nrm((L, RWKV_WIDTH), 0.1),
        "a2": nrm((L, A_LORA, RWKV_WIDTH), 0.5 * A_LORA ** -0.5),
        "g2": nrm((L, 2, GATE_LORA, RWKV_WIDTH), GATE_LORA ** -0.5),
        "k_k": 0.85 + nrm((L, RWKV_WIDTH), 0.05),
        "k_a": gain((L, RWKV_WIDTH)),
        "r_k": nrm((L, RWKV_HEADS, RWKV_HEAD), 0.1),
        "ln_x_g": gain((L, RWKV_WIDTH)),
        "ln_x_b": nrm((L, RWKV_WIDTH), 0.02),
        "w_o": nrm((L, MIX_WIDTH, D), MLA_WIDTH ** -0.5),
        "w_merge": nrm((L, D, D), D ** -0.5),
        "norm_ffn_g": gain((L, D)),
        "w_ffn_gate": nrm((L, D, FFN_HIDDEN), D ** -0.5),
        "w_ffn_up": nrm((L, D, FFN_HIDDEN), D ** -0.5),
        "ffn_conv_w": nrm((L, CONV_WIDTH, FFN_HIDDEN), CONV_WIDTH ** -0.5),
        "ffn_conv_b": nrm((L, FFN_HIDDEN), 0.02),
        "w_ffn_down": nrm((L, FFN_HIDDEN, D), FFN_HIDDEN ** -0.5),
    }


def reference(x, positions, norm_mix_g, w_in, b_gate, q_a_norm_g, kv_a_norm_g, w_uq, w_ukv,
              qn_norm_g, qr_norm_g, kn_norm_g, kr_norm_g, shift_mu, w0, w2, a0, a2, g2,
              k_k, k_a, r_k, ln_x_g, ln_x_b, w_o, w_merge, norm_ffn_g, w_ffn_gate, w_ffn_up,
              ffn_conv_w, ffn_conv_b, w_ffn_down):
    B, S, D = x.shape
    inv_freq = ROPE_THETA ** (-jnp.arange(0, QK_ROPE, 2, dtype=jnp.float32) / QK_ROPE)
    ang = positions.astype(jnp.float32)[..., None] * inv_freq
    cos, sin = jnp.cos(ang).astype(x.dtype), jnp.sin(ang).astype(x.dtype)

    for l in range(DEPTH):
        h = rms_norm(x, norm_mix_g[l])
        proj = h @ w_in[l]
        mla_cols, rw_cols, gate_cols = _split(proj, [MLA_COLS, RWKV_COLS, GATE_COLS])
        o_a = mla_branch(mla_cols, cos, sin, q_a_norm_g[l], kv_a_norm_g[l], w_uq[l], w_ukv[l],
                         qn_norm_g[l], qr_norm_g[l], kn_norm_g[l], kr_norm_g[l])
        o_b = rwkv7_branch(rw_cols, shift_mu[l], w0[l], w2[l], a0[l], a2[l], g2[l],
                           k_k[l], k_a[l], r_k[l], ln_x_g[l], ln_x_b[l])
        w_o_l = w_o[l]
        y_a = o_a @ w_o_l[:MLA_WIDTH]
        y_b = o_b @ w_o_l[MLA_WIDTH:]
        gates = jax.nn.sigmoid(gate_cols.reshape(B, S, N_BRANCHES, D) + b_gate[l])
        x = x + (gates[:, :, 0] * y_a + gates[:, :, 1] * y_b) @ w_merge[l]
        h = rms_norm(x, norm_ffn_g[l])
        gp = dwconv_centered(h @ w_ffn_gate[l], ffn_conv_w[l], ffn_conv_b[l])
        x = x + (jax.nn.silu(gp) * (h @ w_ffn_up[l])) @ w_ffn_down[l]
    return x
```

```python
import math
import numpy as np
from contextlib import ExitStack
import concourse.bass as bass
import concourse.mybir as mybir
from concourse.bass_utils import run_bass_kernel_spmd

F32 = mybir.dt.float32
BF16 = mybir.dt.bfloat16
I32 = mybir.dt.int32
AF = mybir.ActivationFunctionType
ALU = mybir.AluOpType
AX = mybir.AxisListType
ENGS = ["pe", "dve", "act", "pool", "sp"]

D = 1024
HEADS = 8
NOPE, ROPE, VH = 64, 32, 64
QLORA, KVLORA = 256, 128
MLA_COLS = 416
RW0 = 416
RW_COLS = 1984
G0 = 2400
IN_COLS = 4448
FFN = 2816
NHC = 22
NORM_EPS = 1e-6
GN_EPS = 64e-5
CH = 64
HORD = [0, 2, 4, 6, 1, 3, 5, 7]
PE_FENCE = False


class Buf:
    __slots__ = ("name", "w", "r", "dsem", "dcnt", "excl")

    def __init__(self, name, excl=False):
        self.name = name
        self.excl = excl
        self.w = None
        self.r = []
        self.dsem = None
        self.dcnt = 0


class Sched:
    NDPOOL = 64

    def __init__(self, nc, gstack):
        self.nc = nc
        self.bufs = []
        self.phase = 0
        self.stack = None
        self.esem = {e: gstack.enter_context(nc.semaphore(f"se_{e}")) for e in ENGS}
        self.dpool = [gstack.enter_context(nc.semaphore(f"sd_{i}")) for i in range(self.NDPOOL)]

    def buf(self, name="b"):
        b = Buf(name, excl=name.startswith("ps"))
        self.bufs.append(b)
        return b

    def bufs_n(self, n, name="b"):
        return [self.buf(f"{name}{i}") for i in range(n)]

    def begin(self):
        self.phase += 1
        self.stack = ExitStack()
        self.prog = {e: [] for e in ENGS}
        self.cnt = {e: 0 for e in ENGS}
        self.sem = dict(self.esem)
        self.waited = {e: {} for e in ENGS}
        self.dbufs = []
        for b in self.bufs:
            b.w = None
            b.r = []
            b.dsem = None
            b.dcnt = 0
        return self.stack

    def sb(self, name, shape, dt):
        return self.stack.enter_context(
            self.nc.sbuf_tensor(f"{name}_{self.phase}", shape, dt))

    def ps(self, name, shape, dt):
        return self.stack.enter_context(
            self.nc.psum_tensor(f"{name}_{self.phase}", shape, dt))

    def _waits(self, eng, reads, writes):
        toks = []
        for b in reads:
            if b.w is not None:
                toks.append(b.w)
            if b.excl:
                toks.extend(t for t in b.r if not (t[0] == "e" and t[1] == eng))
        for b in writes:
            if b.w is not None:
                toks.append(b.w)
            toks.extend(b.r)
        waits = []
        for t in toks:
            if t[0] == "e":
                _, e2, val = t
                if e2 == eng and eng == "pe":
                    continue
                key = ("e", e2)
                sem = self.sem[e2]
            else:
                b = t[1]
                key = ("d", id(b))
                sem = b.dsem
                val = b.dcnt
            if self.waited[eng].get(key, 0) >= val:
                continue
            self.waited[eng][key] = val
            waits.append((sem, val))
        return waits

    def op(self, eng, fn, reads=(), writes=()):
        waits = self._waits(eng, reads, writes)
        if eng == "pe" and getattr(self, "fence", False):
            self.fence = False
            if self.cnt["pe"] > 0:
                waits.append((self.sem["pe"], self.cnt["pe"]))
        self.cnt[eng] += 1
        tok = ("e", eng, self.cnt[eng])
        self.prog[eng].append((waits, fn, (self.sem[eng], 1)))
        for b in reads:
            b.r.append(tok)
        for b in writes:
            b.w = tok
            b.r = []

    def pe_fence(self):
        self.fence = True

    def dma(self, eng, out, in_, sbuf, reads=(), writes=(), **kw):
        waits = self._waits(eng, reads, writes)
        if sbuf.dsem is None:
            assert len(self.dbufs) < self.NDPOOL, "out of DMA semaphores"
            sbuf.dsem = self.dpool[len(self.dbufs)]
            self.dbufs.append(sbuf)
        sbuf.dcnt += 16
        tok = ("d", sbuf)
        self.prog[eng].append(
            (waits, lambda e: e.dma_start(out=out, in_=in_, **kw), (sbuf.dsem, 16)))
        for b in reads:
            b.r.append(tok)
        for b in writes:
            b.w = tok
            b.r = []

    def end(self):
        waits = []
        for b in self.dbufs:
            if self.waited["sp"].get(("d", id(b)), 0) < b.dcnt:
                waits.append((b.dsem, b.dcnt))
        self.prog["sp"].append((waits, None, None))
        prog = self.prog

        def mk(e):
            def f(engine):
                for waits, fn, inc in prog[e]:
                    for sem, val in waits:
                        engine.wait_ge(sem, val)
                    if fn is not None:
                        fn(engine).then_inc(inc[0], inc[1])
            return f

        allsems = [self.sem[e] for e in ENGS] + [b.dsem for b in self.dbufs]

        def clr(engine):
            for sm in allsems:
                engine.sem_clear(sm)

        def nop_(engine):
            pass

        with self.nc.Block() as block0:
            block0.tensor(nop_)
            block0.vector(nop_)
            block0.scalar(nop_)
            block0.gpsimd(nop_)
            block0.sync(clr)
        with self.nc.Block() as block:
            block.tensor(mk("pe"))
            block.vector(mk("dve"))
            block.scalar(mk("act"))
            block.gpsimd(mk("pool"))
            block.sync(mk("sp"))
        self.stack.close()
        self.stack = None

    def mm(self, out, lhsT, rhs, start=True, stop=True, reads=(), writes=()):
        self.op("pe", lambda e: e.matmul(out, lhsT, rhs, start=start, stop=stop),
                reads, writes)

    def tr(self, out, in_, ident, reads=(), writes=()):
        self.op("pe", lambda e: e.transpose(out, in_, ident), reads, writes)

    def act(self, out, in_, func, reads=(), writes=(), **kw):
        self.op("act", lambda e: e.activation(out, in_, func, **kw), reads, writes)

    def copy(self, eng, out, in_, reads=(), writes=()):
        if eng == "act":
            self.op(eng, lambda e: e.copy(out, in_), reads, writes)
        else:
            self.op(eng, lambda e: e.tensor_copy(out, in_), reads, writes)

    def tt(self, eng, out, in0, in1, op, reads=(), writes=()):
        self.op(eng, lambda e: e.tensor_tensor(out, in0, in1, op), reads, writes)

    def ts(self, eng, out, in0, s1, s2, op0, op1=None, reads=(), writes=()):
        if op1 is None:
            self.op(eng, lambda e: e.tensor_scalar(out, in0, s1, None, op0), reads, writes)
        else:
            self.op(eng, lambda e: e.tensor_scalar(out, in0, s1, s2, op0, op1), reads, writes)

    def stt(self, out, in0, scalar, in1, op0, op1, reads=(), writes=()):
        self.op("dve", lambda e: e.scalar_tensor_tensor(out, in0, scalar, in1, op0, op1),
                reads, writes)

    def red(self, out, in_, reads=(), writes=()):
        self.op("dve", lambda e: e.tensor_reduce(out, in_, AX.X, ALU.add), reads, writes)

    def recip(self, out, in_, reads=(), writes=()):
        self.op("dve", lambda e: e.reciprocal(out, in_), reads, writes)

    def memset(self, eng, ap, val, writes=()):
        self.op(eng, lambda e: e.memset(ap, val), (), writes)


def bc(ap, axis, shape):
    return ap.unsqueeze(axis).to_broadcast(shape)


def build(S_LEN=4096, dbg=False, stop_after=None, rbf16=True, ibf16=True):
    NT = S_LEN // 128
    NB = S_LEN // 512
    NCK = S_LEN // CH
    nc = bass.Bass("TRN2", target_bir_lowering=False)
    okind = "ExternalOutput" if dbg else "Internal"

    def din(name, shape, dt=F32):
        return nc.dram_tensor(name, shape, dt, kind="ExternalInput").ap()

    def dscr(name, shape, dt=F32):
        return nc.dram_tensor(name, shape, dt, kind=okind).ap()

    x = din("x", [S_LEN, D])
    pos = din("positions", [S_LEN], I32)
    invf = din("inv_freq", [16])
    norm_mix_g = din("norm_mix_g", [D])
    w_in = din("w_in", [D, IN_COLS])
    b_gate = din("b_gate", [2, D])
    q_a_norm_g = din("q_a_norm_g", [QLORA])
    kv_a_norm_g = din("kv_a_norm_g", [KVLORA])
    w_uq = din("w_uq", [QLORA, 768])
    w_ukv = din("w_ukv", [KVLORA, 1024])
    qn_g = din("qn_norm_g", [NOPE])
    qr_g = din("qr_norm_g", [ROPE])
    kn_g = din("kn_norm_g", [NOPE])
    kr_g = din("kr_norm_g", [ROPE])
    shift_mu = din("shift_mu", [2, RW_COLS])
    w0 = din("w0", [2, 512])
    w2 = din("w2", [2, 64, 512])
    a0 = din("a0", [512])
    a2 = din("a2", [64, 512])
    g2 = din("g2", [2, 128, 512])
    k_k = din("k_k", [512])
    k_a = din("k_a", [512])
    r_k = din("r_k", [512])
    ln_g = din("ln_x_g", [512])
    ln_b = din("ln_x_b", [512])
    w_o = din("w_o", [D, D])
    w_merge = din("w_merge", [D, D])
    norm_ffn_g = din("norm_ffn_g", [D])
    w_gate = din("w_ffn_gate", [D, FFN])
    w_up = din("w_ffn_up", [D, FFN])
    conv_w = din("ffn_conv_w", [3, FFN])
    conv_b = din("ffn_conv_b", [FFN])
    w_down = din("w_ffn_down", [FFN, D])
    y = nc.dram_tensor("y", [S_LEN, D], F32, kind="ExternalOutput").ap()

    prw = dscr("prw", [16, 128, S_LEN])
    gat = dscr("gat", [16, 128, S_LEN], BF16)
    qT = dscr("qT", [HEADS, 96, S_LEN], BF16)
    kT = dscr("kT", [HEADS, 96, S_LEN], BF16)
    vaug = dscr("vaug", [S_LEN, HEADS * 65], BF16)
    oaT = dscr("oaT", [HEADS, 64, S_LEN], BF16)
    ofT = dscr("ofT", [4, 128, S_LEN], BF16)
    obT = dscr("obT", [4, 128, S_LEN], BF16)
    x1 = dscr("x1", [S_LEN, D])
    h2T = dscr("h2T", [8, 128, S_LEN], BF16)
    actT = dscr("actT", [NT, 128, NHC, 128], BF16)

    gst = ExitStack()
    S = Sched(nc, gst)

    def gsb(name, shape, dt):
        return gst.enter_context(nc.sbuf_tensor(name, shape, dt))

    ident = gsb("ident", [128, 128], F32)
    identb = gsb("identb", [128, 128], BF16)
    ones = gsb("ones", [128, 128], F32)

    def load_col(eng, dst, src_vec, b, n=128):
        S.dma(eng, dst, src_vec.rearrange("(p o) -> p o", o=1), b, writes=[b])

    S.begin()
    b0 = S.buf()
    S.memset("pool", ones[:], 1.0, writes=[b0])
    S.op("pool", lambda e: e.affine_select(ident[:], ones[:], [[1, 128]], ALU.is_equal, 0.0,
                                           base=0, channel_multiplier=-1),
         reads=[b0], writes=[b0])
    S.copy("dve", identb[:], ident[:], reads=[b0], writes=[b0])
    S.end()

    def phase_a():
        S.begin()
        win = S.sb("win", [128, 8, IN_COLS], BF16)
        NG = 32
        GW = IN_COLS // NG
        wst = [S.sb(f"wst{i}", [128, 8, GW], F32) for i in range(2)]
        bwst = S.bufs_n(2, "wst")
        bwin = S.buf("win")
        w_in_v = w_in.rearrange("(kc p) n -> p kc n", p=128)
        for gi in range(NG):
            sl = gi % 2
            S.dma("sp", wst[sl][:], w_in_v[:, :, gi * GW:(gi + 1) * GW], bwst[sl], writes=[bwst[sl]])
            eng = ["dve", "pool"][gi % 2]
            S.copy(eng, win[:, :, gi * GW:(gi + 1) * GW], wst[sl][:], reads=[bwst[sl]], writes=[bwin])
        bsm = S.buf("small")
        wuq_st = S.sb("wuq_st", [128, 2, 768], F32)
        wuq = S.sb("wuq", [128, 2, 768], BF16)
        S.dma("sp", wuq_st[:], w_uq.rearrange("(kc p) n -> p kc n", p=128), bsm, writes=[bsm])
        S.copy("dve", wuq[:], wuq_st[:], reads=[bsm], writes=[bsm])
        wukv_st = S.sb("wukv_st", [128, 2, 8, 64], F32)
        wukv = S.sb("wukv", [128, 2, 8, 64], BF16)
        bsm2 = S.buf("small2")
        wukv_v = w_ukv.rearrange("k (h t d) -> k t h d", h=8, t=2, d=64)
        for tt_ in range(2):
            S.dma("sp", wukv_st[:, tt_, :, :], wukv_v[:, tt_, :, :], bsm2, writes=[bsm2])
        S.copy("dve", wukv[:], wukv_st[:], reads=[bsm2], writes=[bsm2])
        g_bc = S.sb("g_bc", [128, D], F32)
        gq_bc = S.sb("gq_bc", [128, QLORA], F32)
        gkv_bc = S.sb("gkv_bc", [128, KVLORA], F32)
        gkr_bc = S.sb("gkr_bc", [128, ROPE], F32)
        gqh_bc = S.sb("gqh_bc", [128, 96], F32)
        gkn_bc = S.sb("gkn_bc", [128, NOPE], F32)
        bg = S.buf("gains")
        S.dma("sp", g_bc[:], norm_mix_g.partition_broadcast(128), bg, writes=[bg])
        S.dma("sp", gq_bc[:], q_a_norm_g.partition_broadcast(128), bg, writes=[bg])
        S.dma("sp", gkv_bc[:], kv_a_norm_g.partition_broadcast(128), bg, writes=[bg])
        S.dma("sp", gkr_bc[:], kr_g.partition_broadcast(128), bg, writes=[bg])
        S.dma("sp", gqh_bc[:, 0:64], qn_g.partition_broadcast(128), bg, writes=[bg])
        S.dma("sp", gqh_bc[:, 64:96], qr_g.partition_broadcast(128), bg, writes=[bg])
        S.dma("sp", gkn_bc[:], kn_g.partition_broadcast(128), bg, writes=[bg])
        invf_bc = S.sb("invf_bc", [128, 16], F32)
        S.dma("sp", invf_bc[:], invf.partition_broadcast(128), bg, writes=[bg])
        bgate = S.sb("bgate", [128, 16], F32)
        bgf = b_gate.rearrange("a d -> (a d)")
        for j in range(16):
            load_col("sp", bgate[:, j:j + 1], bgf[j * 128:(j + 1) * 128], bg)
        posi = S.sb("posi", [128, NT], I32)
        posf = S.sb("posf", [128, NT], F32)
        bpos = S.buf("pos")
        for t in range(NT):
            load_col("sp", posi[:, t:t + 1], pos[t * 128:(t + 1) * 128], bpos)
        S.copy("dve", posf[:], posi[:], reads=[bpos], writes=[bpos])
        ang = S.sb("ang", [128, NT, 16], F32)
        kf = S.sb("kf", [128, NT, 16], F32)
        ki = S.sb("ki", [128, NT, 16], I32)
        rr = S.sb("rr", [128, NT, 16], F32)
        rc = ang
        mk = kf
        cos2 = S.sb("cos2", [128, NT, 32], F32)
        sin2 = S.sb("sin2", [128, NT, 32], F32)
        brp = S.buf("rope")
        TWO_PI = 2.0 * math.pi
        C1 = 6.28125
        C2 = TWO_PI - C1
        sh3 = [128, NT, 16]
        S.tt("dve", ang[:], bc(posf[:], 2, sh3), bc(invf_bc[:], 1, sh3), ALU.mult, reads=[bpos, bg], writes=[brp])
        S.ts("dve", kf[:], ang[:], 1.0 / TWO_PI, None, ALU.mult, reads=[brp], writes=[brp])
        S.copy("dve", ki[:], kf[:], reads=[brp], writes=[brp])
        S.copy("dve", kf[:], ki[:], reads=[brp], writes=[brp])
        S.stt(rr[:], kf[:], -C1, ang[:], ALU.mult, ALU.add, reads=[brp], writes=[brp])
        S.stt(rr[:], kf[:], -C2, rr[:], ALU.mult, ALU.add, reads=[brp], writes=[brp])
        PI_S = 3.1415925
        S.ts("dve", rr[:], rr[:], PI_S, -PI_S, ALU.min, ALU.max, reads=[brp], writes=[brp])
        S.act(sin2[:, :, 0:16], rr[:], AF.Sin, reads=[brp], writes=[brp])
        S.ts("dve", rc[:], rr[:], math.pi / 2, None, ALU.add, reads=[brp], writes=[brp])
        S.ts("dve", mk[:], rc[:], math.pi, None, ALU.is_gt, reads=[brp], writes=[brp])
        S.stt(rc[:], mk[:], -TWO_PI, rc[:], ALU.mult, ALU.add, reads=[brp], writes=[brp])
        S.ts("dve", rc[:], rc[:], PI_S, -PI_S, ALU.min, ALU.max, reads=[brp], writes=[brp])
        S.act(cos2[:, :, 0:16], rc[:], AF.Sin, reads=[brp], writes=[brp])
        S.copy("dve", sin2[:, :, 16:32], sin2[:, :, 0:16], reads=[brp], writes=[brp])
        S.copy("dve", cos2[:, :, 16:32], cos2[:, :, 0:16], reads=[brp], writes=[brp])

        xs = [S.sb(f"xs{i}", [128, D], F32) for i in range(2)]
        bxs = S.bufs_n(2, "xs")
        junk = S.sb("junk", [128, D], BF16)
        bjunk = S.buf("junk")
        hn = S.sb("hn", [128, D], BF16)
        bhn = S.buf("hn")
        st = S.sb("st", [128, 8], F32)
        bst = S.buf("st")
        hT = [S.sb(f"hT{i}", [128, 8, 512], BF16) for i in range(2)]
        bhT = S.bufs_n(2, "hT")
        pT = S.ps("pT", [128, 8, 128], BF16)
        bpT = S.buf("ps_pT")
        pm = S.ps("pm", [128, 512], F32)
        bpm = S.buf("ps_pm")
        ptr = S.ps("ptr", [128, 8, 128], BF16)
        bptr = S.buf("ps_ptr")
        pq = S.ps("pq", [128, 2, 512], F32)
        bpq = S.buf("ps_pq")
        pf = [S.ps(f"pf{i}", [128, 512], F32) for i in range(2)]
        bpf = S.bufs_n(2, "ps_pf")
        cn = S.sb("cn", [128, 384], BF16)
        bcn = S.buf("cn")
        krn = S.sb("krn", [128, 32], F32)
        krA = S.sb("krA", [128, 32], F32)
        krB = S.sb("krB", [128, 32], F32)
        krf = S.sb("krf", [128, 32], BF16)
        bkr = S.buf("kr")
        cT = S.sb("cT", [128, 3, 128], BF16)
        bcT = S.buf("cT")
        qsb = S.sb("qsb", [128, 8, 96], F32)
        bq = S.buf("q")
        sq = S.sb("sq", [128, 8, 96], F32)
        bsq = S.buf("sq")
        s16 = S.sb("s16", [128, 16], F32)
        bs16 = S.buf("s16")
        qA = S.sb("qA", [128, 8, 32], F32)
        qB = S.sb("qB", [128, 8, 32], F32)
        qf = S.sb("qf", [128, 8, 96], BF16)
        bqf = S.buf("qf")
        ksb = S.sb("ksb", [128, 8, 64], F32)
        bk = S.buf("k")
        kfin = S.sb("kfin", [128, 8, 96], BF16)
        bkf = S.buf("kf")
        vt = [S.sb(f"vt{i}", [128, 8, 65], BF16) for i in range(2)]
        bvt = S.bufs_n(2, "vt")
        for i in range(2):
            S.memset("pool", vt[i][:], 1.0, writes=[bvt[i]])
        qTs = [S.sb(f"qTs{i}", [96, 8, 128], BF16) for i in range(2)]
        bqT = S.bufs_n(2, "qTs")
        kTs = [S.sb(f"kTs{i}", [96, 8, 128], BF16) for i in range(2)]
        bkT = S.bufs_n(2, "kTs")
        rws = [S.sb(f"rws{i}", [128, 512], F32) for i in range(3)]
        brws = S.bufs_n(3, "rws")
        gts = [S.sb(f"gts{i}", [128, 512], BF16) for i in range(3)]
        bgts = S.bufs_n(3, "gts")
        qT_v = qT.rearrange("h d s -> d h s")
        kT_v = kT.rearrange("h d s -> d h s")

        rwch = [(RW0 + j * 128, 128) for j in range(12)]
        rwch += [(RW0 + 1536, 128), (RW0 + 1664, 64), (RW0 + 1728, 128), (RW0 + 1856, 128)]

        def tile_gen(blk):
            hb = hT[blk % 2]
            bhb = bhT[blk % 2]
            for tl in range(4):
                t = blk * 4 + tl
                xb = xs[t % 2]
                bx = bxs[t % 2]
                S.dma("sp", xb[:], x[t * 128:(t + 1) * 128, :], bx, writes=[bx])
                S.act(junk[:], xb[:], AF.Square, reads=[bx], writes=[bjunk, bst], accum_out=st[:, 0:1])
                S.act(st[:, 0:1], st[:, 0:1], AF.Sqrt, reads=[bst], writes=[bst], scale=1.0 / D, bias=NORM_EPS)
                S.recip(st[:, 0:1], st[:, 0:1], reads=[bst], writes=[bst])
                S.stt(hn[:], xb[:], st[:, 0:1], g_bc[:], ALU.mult, ALU.mult, reads=[bx, bst, bg], writes=[bhn])
                for kc in range(8):
                    S.tr(pT[:, kc, :], hn[:, kc * 128:(kc + 1) * 128], identb[:], reads=[bhn], writes=[bpT])
                S.copy("act", hb[:, :, tl * 128:(tl + 1) * 128], pT[:], reads=[bpT], writes=[bhb])
                yield
                for kc in range(8):
                    S.mm(pm[:, 0:MLA_COLS], hb[:, kc, tl * 128:(tl + 1) * 128], win[:, kc, 0:MLA_COLS],
                         start=(kc == 0), stop=(kc == 7), reads=[bhb, bwin], writes=[bpm])
                yield
                S.act(junk[:, 0:256], pm[:, 0:256], AF.Square, reads=[bpm], writes=[bjunk, bst], accum_out=st[:, 1:2])
                S.act(junk[:, 0:128], pm[:, 256:384], AF.Square, reads=[bpm], writes=[bjunk, bst], accum_out=st[:, 2:3])
                S.act(junk[:, 0:32], pm[:, 384:416], AF.Square, reads=[bpm], writes=[bjunk, bst], accum_out=st[:, 3:4])
                S.act(st[:, 1:2], st[:, 1:2], AF.Sqrt, reads=[bst], writes=[bst], scale=1.0 / QLORA, bias=NORM_EPS)
                S.act(st[:, 2:3], st[:, 2:3], AF.Sqrt, reads=[bst], writes=[bst], scale=1.0 / KVLORA, bias=NORM_EPS)
                S.act(st[:, 3:4], st[:, 3:4], AF.Sqrt, reads=[bst], writes=[bst], scale=1.0 / ROPE, bias=NORM_EPS)
                S.recip(st[:, 1:4], st[:, 1:4], reads=[bst], writes=[bst])
                S.stt(cn[:, 0:256], pm[:, 0:256], st[:, 1:2], gq_bc[:], ALU.mult, ALU.mult, reads=[bpm, bst], writes=[bcn])
                S.stt(cn[:, 256:384], pm[:, 256:384], st[:, 2:3], gkv_bc[:], ALU.mult, ALU.mult, reads=[bpm, bst], writes=[bcn])
                S.stt(krn[:], pm[:, 384:416], st[:, 3:4], gkr_bc[:], ALU.mult, ALU.mult, reads=[bpm, bst], writes=[bkr])
                S.tt("dve", krA[:], krn[:], cos2[:, t, :], ALU.mult, reads=[bkr, brp], writes=[bkr])
                S.tt("dve", krB[:], krn[:], sin2[:, t, :], ALU.mult, reads=[bkr, brp], writes=[bkr])
                S.tt("dve", krf[:, 0:16], krA[:, 0:16], krB[:, 16:32], ALU.subtract, reads=[bkr], writes=[bkr])
                S.tt("dve", krf[:, 16:32], krA[:, 16:32], krB[:, 0:16], ALU.add, reads=[bkr], writes=[bkr])
                for j in range(3):
                    S.tr(ptr[:, j, :], cn[:, j * 128:(j + 1) * 128], identb[:], reads=[bcn], writes=[bptr])
                S.copy("act", cT[:], ptr[:, 0:3, :], reads=[bptr], writes=[bcT])
                yield
                for hf in range(2):
                    for kc in range(2):
                        S.mm(pq[:, hf, 0:384], cT[:, kc, :], wuq[:, kc, hf * 384:(hf + 1) * 384],
                             start=(kc == 0), stop=(kc == 1), reads=[bcT, bsm], writes=[bpq])
                yield
                for hf in range(2):
                    S.copy("act", qsb[:, hf * 4:(hf + 1) * 4, :],
                           pq[:, hf, 0:384].rearrange("p (h d) -> p h d", d=96), reads=[bpq], writes=[bq])
                S.tt("dve", sq[:], qsb[:], qsb[:], ALU.mult, reads=[bq], writes=[bsq])
                S.red(s16[:, 0:8], sq[:, :, 0:64], reads=[bsq], writes=[bs16])
                S.red(s16[:, 8:16], sq[:, :, 64:96], reads=[bsq], writes=[bs16])
                S.act(s16[:, 0:8], s16[:, 0:8], AF.Sqrt, reads=[bs16], writes=[bs16], scale=1.0 / NOPE, bias=NORM_EPS)
                S.act(s16[:, 8:16], s16[:, 8:16], AF.Sqrt, reads=[bs16], writes=[bs16], scale=1.0 / ROPE, bias=NORM_EPS)
                S.recip(s16[:], s16[:], reads=[bs16], writes=[bs16])
                S.tt("dve", qsb[:, :, 0:64], qsb[:, :, 0:64], bc(s16[:, 0:8], 2, [128, 8, 64]), ALU.mult, reads=[bq, bs16], writes=[bq])
                S.tt("dve", qsb[:, :, 64:96], qsb[:, :, 64:96], bc(s16[:, 8:16], 2, [128, 8, 32]), ALU.mult, reads=[bq, bs16], writes=[bq])
                S.tt("dve", qsb[:], qsb[:], bc(gqh_bc[:], 1, [128, 8, 96]), ALU.mult, reads=[bq, bg], writes=[bq])
                S.copy("act", qf[:, :, 0:64], qsb[:, :, 0:64], reads=[bq], writes=[bqf])
                S.tt("dve", qA[:], qsb[:, :, 64:96], bc(cos2[:, t, :], 1, [128, 8, 32]), ALU.mult, reads=[bq, brp], writes=[bsq])
                S.tt("dve", qB[:], qsb[:, :, 64:96], bc(sin2[:, t, :], 1, [128, 8, 32]), ALU.mult, reads=[bq, brp], writes=[bsq])
                S.tt("dve", qf[:, :, 64:80], qA[:, :, 0:16], qB[:, :, 16:32], ALU.subtract, reads=[bsq], writes=[bqf])
                S.tt("dve", qf[:, :, 80:96], qA[:, :, 16:32], qB[:, :, 0:16], ALU.add, reads=[bsq], writes=[bqf])
                for hf in range(2):
                    S.mm(pq[:, hf, :], cT[:, 2, :], wukv[:, hf, :, :].rearrange("k h d -> k (h d)"),
                         reads=[bcT, bsm2, bq], writes=[bpq])
                yield
                vb = vt[t % 2]
                bv = bvt[t % 2]
                S.copy("act", ksb[:], pq[:, 0, :].rearrange("p (h d) -> p h d", d=64), reads=[bpq], writes=[bk])
                S.copy("act", vb[:, :, 0:64], pq[:, 1, :].rearrange("p (h d) -> p h d", d=64), reads=[bpq], writes=[bv])
                S.dma("sp", vaug[t * 128:(t + 1) * 128, :], vb[:].rearrange("p h d -> p (h d)"), bv, reads=[bv])
                S.tt("dve", sq[:, :, 0:64], ksb[:], ksb[:], ALU.mult, reads=[bk], writes=[bsq])
                S.red(s16[:, 0:8], sq[:, :, 0:64], reads=[bsq], writes=[bs16])
                S.act(s16[:, 0:8], s16[:, 0:8], AF.Sqrt, reads=[bs16], writes=[bs16], scale=1.0 / NOPE, bias=NORM_EPS)
                S.recip(s16[:, 0:8], s16[:, 0:8], reads=[bs16], writes=[bs16])
                S.tt("dve", ksb[:], ksb[:], bc(s16[:, 0:8], 2, [128, 8, 64]), ALU.mult, reads=[bk, bs16], writes=[bk])
                S.tt("dve", kfin[:, :, 0:64], ksb[:], bc(gkn_bc[:], 1, [128, 8, 64]), ALU.mult, reads=[bk, bg], writes=[bkf])
                S.copy("dve", kfin[:, :, 64:96], bc(krf[:], 1, [128, 8, 32]), reads=[bkr], writes=[bkf])
                qo = qTs[t % 2]
                bqo = bqT[t % 2]
                ko = kTs[t % 2]
                bko = bkT[t % 2]
                for h in range(8):
                    S.tr(ptr[0:96, h, :], qf[:, h, :], identb[:], reads=[bqf], writes=[bptr])
                S.copy("act", qo[:], ptr[0:96, :, :], reads=[bptr], writes=[bqo])
                S.dma("sp", qT_v[:, :, t * 128:(t + 1) * 128], qo[:], bqo, reads=[bqo])
                yield
                for h in range(8):
                    S.tr(ptr[0:96, h, :], kfin[:, h, :], identb[:], reads=[bkf], writes=[bptr])
                S.copy("act", ko[:], ptr[0:96, :, :], reads=[bptr], writes=[bko])
                S.dma("sp", kT_v[:, :, t * 128:(t + 1) * 128], ko[:], bko, reads=[bko])
                yield
        def big_gen(blk):
            hb = hT[blk % 2]
            bhb = bhT[blk % 2]
            n = 0
            for j, (c0, wd) in enumerate(rwch):
                p = pf[n % 2]
                bp = bpf[n % 2]
                for kc in range(8):
                    S.mm(p[0:wd, :], win[:, kc, c0:c0 + wd], hb[:, kc, :], start=(kc == 0), stop=(kc == 7),
                         reads=[bwin, bhb], writes=[bp])
                o = rws[n % 3]
                bo = brws[n % 3]
                eng = "dve" if n % 2 == 0 else "act"
                S.copy(eng, o[0:wd, :], p[0:wd, :], reads=[bp], writes=[bo])
                S.dma("sp", prw[j, 0:wd, blk * 512:(blk + 1) * 512], o[0:wd, :], bo, reads=[bo])
                n += 1
                yield
            for j in range(16):
                c0 = G0 + j * 128
                p = pf[n % 2]
                bp = bpf[n % 2]
                for kc in range(8):
                    S.mm(p[:, :], win[:, kc, c0:c0 + 128], hb[:, kc, :], start=(kc == 0), stop=(kc == 7),
                         reads=[bwin, bhb], writes=[bp])
                o = gts[n % 3]
                bo = bgts[n % 3]
                S.act(o[:], p[:], AF.Sigmoid, reads=[bp, bg], writes=[bo], bias=bgate[:, j:j + 1])
                S.dma("sp", gat[j, :, blk * 512:(blk + 1) * 512], o[:], bo, reads=[bo])
                n += 1
                yield
        def run_rr(gens):
            gens = list(gens)
            while gens:
                for g in list(gens):
                    try:
                        next(g)
                    except StopIteration:
                        gens.remove(g)

        for blk in range(NB):
            run_rr([tile_gen(blk)] + ([big_gen(blk - 1)] if blk > 0 else []))
        run_rr([big_gen(NB - 1)])
        S.end()

    phase_a()
    if stop_after == "A":
        gst.close()
        return nc

    def phase_att():
        S.begin()
        va = S.sb("va", [128, NT, HEADS * 65], BF16)
        bva = S.buf("va")
        va_v = vaug.rearrange("(t p) c -> p t c", p=128)
        for t4 in range(0, NT, 4):
            S.dma("sp", va[:, t4:t4 + 4, :], va_v[:, t4:t4 + 4, :], bva, writes=[bva])
        kh = [S.sb(f"kh{i}", [96, S_LEN], BF16) for i in range(2)]
        bkh = S.bufs_n(2, "kh")
        qh = [S.sb(f"qh{i}", [96, S_LEN], BF16) for i in range(2)]
        bqh = S.bufs_n(2, "qh")
        NPS = 5
        pS = [S.ps(f"pS{i}", [128, 512], F32) for i in range(NPS)]
        bpS = S.bufs_n(NPS, "ps_S")
        NPT = 6
        pt = [S.sb(f"pt{i}", [128, 512], BF16) for i in range(NPT)]
        bpt = S.bufs_n(NPT, "pt")
        pacc = [S.ps(f"pacc{i}", [128, 512], F32) for i in range(2)]
        bacc = S.bufs_n(2, "ps_acc")
        pbc = S.ps("pbc", [64, 512], F32)
        bpbc = S.buf("ps_bc")
        rden = S.sb("rden", [128, 512], F32)
        brden = S.buf("rden")
        bcs = S.sb("bcs", [64, 512], F32)
        bbcs = S.buf("bcs")
        oas = [S.sb(f"oas{i}", [64, 512], BF16) for i in range(2)]
        boas = S.bufs_n(2, "oas")
        scale = 1.0 / math.sqrt(NOPE + ROPE)
        it = 0
        for h in range(HEADS):
            k_ = kh[h % 2]
            bk_ = bkh[h % 2]
            q_ = qh[h % 2]
            bq_ = bqh[h % 2]
            S.dma("sp", k_[:], kT[h, :, :], bk_, writes=[bk_])
            S.dma("sp", q_[:], qT[h, :, :], bq_, writes=[bq_])
            for qb in range(NB):
                acc = pacc[it % 2]
                ba_ = bacc[it % 2]
                it += 1
                qsl = q_[:, qb * 512:(qb + 1) * 512]

                def qk(kt):
                    ps_ = pS[kt % NPS]
                    S.mm(ps_[:], k_[:, kt * 128:(kt + 1) * 128], qsl, reads=[bk_, bq_], writes=[bpS[kt % NPS]])

                def pv(kt):
                    ps_ = pS[kt % NPS]
                    p_ = pt[kt % NPT]
                    S.act(p_[:], ps_[:], AF.Exp, reads=[bpS[kt % NPS]], writes=[bpt[kt % NPT]], scale=scale)
                    S.mm(acc[0:65, :], va[:, kt, h * 65:(h + 1) * 65], p_[:], start=(kt == 0), stop=(kt == NT - 1),
                         reads=[bva, bpt[kt % NPT]], writes=[ba_])

                LA = 4
                for kt in range(min(LA, NT)):
                    qk(kt)
                for kt in range(NT):
                    pv(kt)
                    if kt + LA < NT:
                        qk(kt + LA)
                S.recip(rden[64:65, :], acc[64:65, :], reads=[ba_], writes=[brden])
                S.mm(pbc[:], ones[64:65, 0:64], rden[64:65, :], reads=[brden], writes=[bpbc])
                S.copy("act", bcs[:], pbc[:], reads=[bpbc], writes=[bbcs])
                o_ = oas[it % 2]
                bo_ = boas[it % 2]
                S.tt("dve", o_[:], acc[0:64, :], bcs[:], ALU.mult, reads=[ba_, bbcs], writes=[bo_])
                S.dma("sp", oaT[h, :, qb * 512:(qb + 1) * 512], o_[:], bo_, reads=[bo_])
        S.end()

    phase_att()
    if stop_after == "ATT":
        gst.close()
        return nc


    rwrel = [(j * 128, 128) for j in range(12)] + [(1536, 128), (1664, 64), (1728, 128), (1856, 128)]
    RDT = BF16 if rbf16 else F32
    IDT = BF16 if ibf16 else F32
    ard = dscr("ard", [2, 128, 4, NCK, 2, CH], RDT)
    btd = dscr("btd", [2, 128, 4, S_LEN], RDT)
    ktd = dscr("ktd", [2, 128, 4, S_LEN], RDT)
    vrd = dscr("vrd", [128, 4, S_LEN], RDT)
    gamd = dscr("gamd", [2, 128, 4, NCK], F32)
    bond = dscr("bond", [128, 4, S_LEN], F32)
    gtd = dscr("gtd", [2, 128, 4, S_LEN], BF16)
    orawd = dscr("orawd", [2, 128, 4, S_LEN], F32)
    ofT_v = ofT.rearrange("a p s -> p a s")
    obT_v = obT.rearrange("a p s -> p a s")

    def rw_params(extra_vecs):
        bpar = S.buf("par")
        par = S.sb("par", [128, 4 * len(extra_vecs)], F32)
        for ci, vec in enumerate(extra_vecs):
            if vec is None:
                continue
            for p in range(4):
                load_col("sp", par[:, ci * 4 + p:ci * 4 + p + 1], vec[p * 128:(p + 1) * 128], bpar)
        return par, bpar

    def make_blk2(bcst):
        blk2 = S.sb("blk2", [128, 128], F32)
        S.memset("pool", blk2[:], 0.0, writes=[bcst])
        S.memset("pool", blk2[0:64, 0:64], 1.0, writes=[bcst])
        S.memset("pool", blk2[64:128, 64:128], 1.0, writes=[bcst])
        return blk2

    def phase_rprep():
        RB = 256
        NCB = RB // CH
        S.begin()
        par, bpar = rw_params([a0, k_k, k_a, None, w0[0], r_k, w0[1]])
        S.ts("dve", par[:, 12:16], par[:, 8:12], -1.0, 1.0, ALU.mult, ALU.add, reads=[bpar], writes=[bpar])
        mu = S.sb("mu", [128, 16, 3], F32)
        S.memset("pool", mu[:], 0.0, writes=[bpar])
        for j, (c0, wd) in enumerate(rwrel):
            load_col("sp", mu[0:wd, j, 1:2], shift_mu[0, c0:c0 + wd], bpar, n=wd)
            load_col("sp", mu[0:wd, j, 2:3], shift_mu[1, c0:c0 + wd], bpar, n=wd)
        S.ts("dve", mu[:, :, 0], mu[:, :, 1], -1.0, 1.0, ALU.mult, ALU.add, reads=[bpar], writes=[bpar])
        S.tt("dve", mu[:, :, 0], mu[:, :, 0], mu[:, :, 2], ALU.subtract, reads=[bpar], writes=[bpar])
        bw = S.buf("w")
        a2_sb = S.sb("a2_sb", [64, 512], F32)
        w2_sb = S.sb("w2_sb", [128, 512], F32)
        g2_sb = [S.sb(f"g2_sb{d}", [128, 512], F32) for d in range(2)]
        S.dma("sp", a2_sb[:], a2[:, :], bw, writes=[bw])
        S.dma("sp", w2_sb[64:128, :], w2[0, :, :], bw, writes=[bw])
        S.dma("sp", w2_sb[0:64, :], w2[1, :, :], bw, writes=[bw])
        for d in range(2):
            S.dma("sp", g2_sb[d][:], g2[d, :, :], bw, writes=[bw])
        bcst = S.buf("const")
        blk2 = make_blk2(bcst)
        blk_rk = S.sb("blk_rk", [128, 4, 128], F32)
        for p in range(4):
            S.ts("dve", blk_rk[:, p, :], blk2[:], par[:, 20 + p:21 + p], None, ALU.mult, reads=[bcst, bpar], writes=[bcst])
        rmask = S.sb("rmask", [128, 4 * RB], F32)
        S.memset("pool", rmask[:], 1.0, writes=[bcst])
        S.op("pool", lambda e: e.affine_select(rmask[:].rearrange("p (a b) -> p a b", b=CH), rmask[:].rearrange("p (a b) -> p a b", b=CH),
                                               [[0, 4 * RB // CH], [1, CH]], ALU.is_gt, 0.0, base=0, channel_multiplier=0),
             reads=[bcst], writes=[bcst])
        prw_v = prw.rearrange("j p s -> p j s")
        sh4 = [128, 4, RB]

        def pcol(c):
            return par[:, c:c + 4].unsqueeze(2).to_broadcast(sh4)

        def v4(ap):
            return ap.rearrange("p a (c t) -> p a c t", t=CH)

        def blk_thread(tid):
            T = f"t{tid}"
            pb = S.sb(T + "pb", [128, 16, RB + 2], F32)
            bpb = S.buf(T + "pb")
            S.memset("pool", pb[:], 0.0, writes=[bpb])
            u = S.sb(T + "u", [128, 16, RB], F32)
            bu, buk, buv = S.buf(T + "u"), S.buf(T + "uk"), S.buf(T + "uv")
            A = [S.sb(T + f"A{i}", [128, 4, RB], F32) for i in range(8)]
            bA = S.bufs_n(8, T + "A")
            vr = S.sb(T + "vr", [128, 4, RB], RDT)
            bvr = S.buf(T + "vr")
            btr = [S.sb(T + f"btr{d}", [128, 4, RB], RDT) for d in range(2)]
            ktr = [S.sb(T + f"ktr{d}", [128, 4, RB], RDT) for d in range(2)]
            gts = [S.sb(T + f"gts{d}", [128, 4, RB], BF16) for d in range(2)]
            ar = [S.sb(T + f"ar{d}", [128, 4, NCB, 2, CH], RDT) for d in range(2)]
            gam = [S.sb(T + f"gam{d}", [128, 4, NCB], F32) for d in range(2)]
            bbtr, bktr, bgts, bar, bgam = [S.bufs_n(2, T + nm) for nm in ("btr", "ktr", "gts", "ar", "gam")]
            tot = S.sb(T + "tot", [128, 4, NCB], F32)
            btot = S.buf(T + "tot")
            pp = [S.ps(T + f"pp{i}", [128, 512], F32) for i in range(4)]
            bpp = S.bufs_n(4, "ps_pp" + T)
            npp = [0]

            def nextpp():
                i = npp[0] % 4
                npp[0] += 1
                return pp[i], bpp[i]

            for blk in range(tid, S_LEN // RB, 2):
                t0 = blk * RB
                c0 = t0 // CH
                lo = max(t0 - 1, 0)
                hi = min(t0 + RB + 1, S_LEN)
                dlo = lo - (t0 - 1)
                if t0 == 0:
                    S.memset("pool", pb[:, :, 0:1], 0.0, writes=[bpb])
                if t0 + RB == S_LEN:
                    S.memset("pool", pb[:, :, RB + 1:RB + 2], 0.0, writes=[bpb])
                S.dma("sp", pb[:, 0:13, dlo:dlo + (hi - lo)], prw_v[:, 0:13, lo:hi], bpb, writes=[bpb])
                S.dma("sp", pb[0:64, 13, dlo:dlo + (hi - lo)], prw_v[0:64, 13, lo:hi], bpb, writes=[bpb])
                S.dma("sp", pb[:, 14:16, dlo:dlo + (hi - lo)], prw_v[:, 14:16, lo:hi], bpb, writes=[bpb])
                S.tt("pool", u[:], pb[:, :, 1:RB + 1], mu[:, :, 0:1].to_broadcast([128, 16, RB]), ALU.mult, reads=[bpb, bpar], writes=[bu, buk, buv])
                for j in range(16):
                    wd = rwrel[j][1]
                    S.stt(u[0:wd, j, :], pb[0:wd, j, 0:RB], mu[0:wd, j, 1:2], u[0:wd, j, :], ALU.mult, ALU.add, reads=[bpb, bpar, bu], writes=[bu, buk, buv])
                    S.stt(u[0:wd, j, :], pb[0:wd, j, 2:RB + 2], mu[0:wd, j, 2:3], u[0:wd, j, :], ALU.mult, ALU.add, reads=[bpb, bpar, bu], writes=[bu, buk, buv])
                yield
                r_ = u[:, 0:4, :]
                k_ = u[:, 4:8, :]
                v_ = u[:, 8:12, :]
                S.copy("pool", vr[:], v_, reads=[buv], writes=[bvr])
                S.dma("sp", vrd[:, :, t0:t0 + RB], vr[:], bvr, reads=[bvr])
                for p in range(4):
                    pq_, bq_ = nextpp()
                    S.mm(pq_[:, 0:RB], a2_sb[0:64, p * 128:(p + 1) * 128], u[0:64, 12, :], reads=[bw, bu], writes=[bq_])
                    S.act(A[0][:, p, :], pq_[:, 0:RB], AF.Sigmoid, reads=[bq_, bpar], writes=[bA[0]], bias=par[:, p:p + 1])
                yield
                S.tt("dve", A[1][:], k_, pcol(4), ALU.mult, reads=[buk, bpar], writes=[bA[1]])
                S.act(A[6][:], A[1][:], AF.Square, reads=[bA[1]], writes=[bA[6]])
                for p in range(4):
                    pq_, bq_ = nextpp()
                    S.mm(pq_[:, 0:RB], blk2[:], A[6][:, p, :], reads=[bcst, bA[6]], writes=[bq_])
                    S.act(A[4][:, p, :], pq_[:, 0:RB], AF.Sqrt, reads=[bq_], writes=[bA[4]])
                S.ts("dve", A[4][:], A[4][:], 1e-12, None, ALU.max, reads=[bA[4]], writes=[bA[4]])
                S.recip(A[4][:], A[4][:], reads=[bA[4]], writes=[bA[4]])
                S.tt("dve", A[1][:], A[1][:], A[4][:], ALU.mult, reads=[bA[1], bA[4]], writes=[bA[1]])
                yield
                S.tt("dve", A[6][:], A[0][:], pcol(8), ALU.mult, reads=[bA[0], bpar], writes=[bA[6]])
                S.tt("dve", A[6][:], A[6][:], pcol(12), ALU.add, reads=[bA[6], bpar], writes=[bA[6]])
                S.tt("dve", k_, k_, A[6][:], ALU.mult, reads=[buk, bA[6]], writes=[buk])
                S.tt("dve", A[3][:], A[1][:], A[0][:], ALU.mult, reads=[bA[1], bA[0]], writes=[bA[3]])
                S.ts("pool", A[1][:], A[1][:], -1.0, None, ALU.mult, reads=[bA[1]], writes=[bA[1]])
                yield
                S.tt("dve", A[6][:], r_, k_, ALU.mult, reads=[bu, buk], writes=[bA[6]])
                for p in range(4):
                    pq_, bq_ = nextpp()
                    S.mm(pq_[:, 0:RB], blk_rk[:, p, :], A[6][:, p, :], reads=[bcst, bA[6]], writes=[bq_])
                    S.tt("dve", A[0][:, p, :], pq_[:, 0:RB], u[:, 8 + p, :], ALU.mult, reads=[bq_, buv], writes=[bA[0]])
                S.dma("sp", bond[:, :, t0:t0 + RB], A[0][:], bA[0], reads=[bA[0]])
                yield
                for d in range(2):
                    dp0 = 64 if d == 0 else 0
                    dl_chunk = 12 if d == 0 else 13
                    gl_chunk = 14 if d == 0 else 15
                    w0c = 16 if d == 0 else 24
                    S.act(A[6][:, 0, :], u[:, gl_chunk, :], AF.Sigmoid, reads=[bu], writes=[bA[6]])
                    for p in range(4):
                        pq_, bq_ = nextpp()
                        S.mm(pq_[:, 0:RB], g2_sb[d][:, p * 128:(p + 1) * 128], A[6][:, 0, :], reads=[bw, bA[6]], writes=[bq_])
                        S.copy("act", gts[d][:, p, :], pq_[:, 0:RB], reads=[bq_], writes=[bgts[d]])
                    S.dma("sp", gtd[d, :, :, t0:t0 + RB], gts[d][:], bgts[d], reads=[bgts[d]])
                    yield
                    S.act(A[6][dp0:dp0 + 64, 1, :], u[dp0:dp0 + 64, dl_chunk, :], AF.Tanh, reads=[bu], writes=[bA[6]])
                    for p in range(4):
                        pq_, bq_ = nextpp()
                        S.mm(pq_[:, 0:RB], w2_sb[dp0:dp0 + 64, p * 128:(p + 1) * 128], A[6][dp0:dp0 + 64, 1, :], reads=[bw, bA[6]], writes=[bq_])
                        S.act(A[4][:, p, :], pq_[:, 0:RB], AF.Sigmoid, reads=[bq_, bpar], writes=[bA[4]], bias=par[:, w0c + p:w0c + p + 1])
                    S.ts("dve", A[4][:], A[4][:], -math.exp(-0.5), None, ALU.mult, reads=[bA[4]], writes=[bA[4]])
                    yield
                    fl = lambda t_: t_[:].rearrange("p a t -> p (a t)")
                    S.op("dve", (lambda o_, m_, i_: lambda e: e.tensor_tensor_scan(o_, m_, i_, 0.0, ALU.mult, ALU.add))(fl(A[5]), rmask[:], fl(A[4])),
                         reads=[bA[4], bcst], writes=[bA[5]])
                    if d == 1:
                        S.copy("dve", tot[:], v4(A[5][:])[:, :, :, CH - 1], reads=[bA[5]], writes=[btot])
                        S.tt("dve", A[5][:], A[4][:], A[5][:], ALU.subtract, reads=[bA[4], bA[5]], writes=[bA[5]])
                        S.tt("dve", v4(A[5][:]), v4(A[5][:]), tot[:].unsqueeze(3).to_broadcast([128, 4, NCB, CH]), ALU.add,
                             reads=[bA[5], btot], writes=[bA[5]])
                    yield
                    S.tt("dve", A[6][:], A[5][:], A[4][:], ALU.subtract, reads=[bA[5], bA[4]], writes=[bA[6]])
                    S.act(A[6][:], A[6][:], AF.Exp, reads=[bA[6]], writes=[bA[6]])
                    S.tt("dve", ar[d][:, :, :, 0, :], v4(A[1][:]), v4(A[6][:]), ALU.mult, reads=[bA[1], bA[6]], writes=[bar[d]])
                    S.act(A[7][:], A[5][:], AF.Exp, reads=[bA[5]], writes=[bA[7]])
                    S.tt("pool", ar[d][:, :, :, 1, :], v4(r_), v4(A[7][:]), ALU.mult, reads=[bu, bA[7]], writes=[bar[d]])
                    gidx = CH - 1 if d == 0 else 0
                    S.copy("dve", gam[d][:], v4(A[7][:])[:, :, :, gidx], reads=[bA[7]], writes=[bgam[d]])
                    S.act(A[6][:], A[5][:], AF.Exp, reads=[bA[5]], writes=[bA[6]], scale=-1.0)
                    S.tt("dve", btr[d][:], A[3][:], A[6][:], ALU.mult, reads=[bA[3], bA[6]], writes=[bbtr[d]])
                    S.tt("pool", ktr[d][:], k_, A[6][:], ALU.mult, reads=[buk, bA[6]], writes=[bktr[d]])
                    yield
                    for p in range(4):
                        S.dma("sp", ard[d, :, p, c0:c0 + NCB, :, :], ar[d][:, p, :, :, :], bar[d], reads=[bar[d]])
                    S.dma("sp", btd[d, :, :, t0:t0 + RB], btr[d][:], bbtr[d], reads=[bbtr[d]])
                    S.dma("sp", ktd[d, :, :, t0:t0 + RB], ktr[d][:], bktr[d], reads=[bktr[d]])
                    S.dma("sp", gamd[d, :, :, c0:c0 + NCB], gam[d][:], bgam[d], reads=[bgam[d]])
            yield

        threads = [blk_thread(0), blk_thread(1)]
        while threads:
            for g in list(threads):
                try:
                    next(g)
                except StopIteration:
                    threads.remove(g)
        S.end()

    def phase_rscan(dirs=(0, 1)):
        RB = 256
        NRB = S_LEN // RB
        NCB = RB // CH
        S.begin()
        bcst = S.buf("const")
        I2 = S.sb("I2", [128, 64], F32)
        for hf in range(2):
            r0 = hf * 64
            S.copy("dve", I2[r0:r0 + 64, :], ident[r0:r0 + 64, r0:r0 + 64], writes=[bcst])
        identr = S.sb("identr", [128, 128], RDT)
        S.copy("dve", identr[:], ident[:], writes=[bcst])

        def dir_thread(d):
            T = f"d{d}"
            bmk = S.buf(T + "mask")
            mask2 = S.sb(T + "mask2", [128, 128], F32)
            maskY = S.sb(T + "maskY", [128, 64], F32)
            for hf in range(2):
                r0 = hf * 64
                on = ones[r0:r0 + 64, 0:64]
                if d == 0:
                    pat, cm = [[1, 64]], -1
                else:
                    pat, cm = [[-1, 64]], 1
                S.op("pool", (lambda o_, i_, pat_, cm_: lambda e: e.affine_select(o_, i_, pat_, ALU.is_gt, 0.0, base=0, channel_multiplier=cm_))(mask2[r0:r0 + 64, 0:64], on, pat, cm), writes=[bmk])
                S.op("pool", (lambda o_, i_, pat_, cm_: lambda e: e.affine_select(o_, i_, pat_, ALU.is_ge, 0.0, base=0, channel_multiplier=cm_))(mask2[r0:r0 + 64, 64:128], on, pat, cm), writes=[bmk])
                pat2 = [[-pat[0][0], 64]]
                S.op("pool", (lambda o_, i_, pat_, cm_: lambda e: e.affine_select(o_, i_, pat_, ALU.is_gt, 0.0, base=0, channel_multiplier=cm_))(maskY[r0:r0 + 64, :], on, pat2, -cm), writes=[bmk])
            Hst = S.sb(T + "Hst", [128, 4, 64], F32)
            Hr = S.sb(T + "Hr", [128, 4, 64], RDT) if RDT != F32 else Hst
            bH = S.buf(T + "H")
            S.memset("pool", Hst[:], 0.0, writes=[bH])
            if RDT != F32:
                S.memset("pool", Hr[:], 0.0, writes=[bH])
            ar_s = [S.sb(f"{T}ar{i}", [128, 4, NCB, 2, CH], RDT) for i in range(2)]
            btr_s = [S.sb(f"{T}btr{i}", [128, 4, RB], RDT) for i in range(2)]
            ktr_s = [S.sb(f"{T}ktr{i}", [128, 4, RB], RDT) for i in range(2)]
            vr_s = [S.sb(f"{T}vr{i}", [128, 4, RB], RDT) for i in range(2)]
            gam_s = [S.sb(f"{T}gam{i}", [128, 4, NCB], F32) for i in range(2)]
            bar_s, bbtr_s, bktr_s, bvr_s, bgam_s = [S.bufs_n(2, T + nm) for nm in ("ar", "btr", "ktr", "vr", "gam")]
            tms = [S.sb(f"{T}tm{i}", [128, NCB, 512], RDT) for i in range(3)]
            btm = S.bufs_n(3, T + "tm")
            A1m = [S.sb(f"{T}A1m{i}", [128, 4, 128], RDT) for i in range(2)]
            A2m = [S.sb(f"{T}A2m{i}", [128, 4, 128], RDT) for i in range(2)]
            bA1m = S.bufs_n(2, T + "A1m")
            bA2m = S.bufs_n(2, T + "A2m")
            Xk = [S.sb(f"{T}Xk{i}", [128, 4, 64], IDT) for i in range(2)]
            Yk = [S.sb(f"{T}Yk{i}", [128, 4, 64], IDT) for i in range(2)]
            IY = [S.sb(f"{T}IY{i}", [128, 4, 64], IDT) for i in range(2)]
            Qk = [S.sb(f"{T}Qk{i}", [128, 4, 64], IDT) for i in range(2)]
            TT = [S.sb(f"{T}TT{i}", [128, 4, 64], RDT) for i in range(2)]
            bXk, bYk, bIY, bQk, bTT = [S.bufs_n(2, T + nm) for nm in ("Xk", "Yk", "IY", "Qk", "TT")]
            Xs = S.sb(T + "Xs", [128, 4, 64], RDT)
            Us = S.sb(T + "Us", [128, 4, 64], RDT)
            bXs, bUs = S.buf(T + "Xs"), S.buf(T + "Us")
            oT = [S.sb(f"{T}oT{i}", [128, 4, RB], F32) for i in range(2)]
            boT = S.bufs_n(2, T + "oT")

            def bank(name):
                return S.ps(T + name, [128, 512], F32)
            bkA, bkY, bkQ, bkS = bank("bA"), bank("bY"), bank("bQ"), bank("bS")
            bpA, bpY, bpQ, bpS = [S.buf("ps_" + T + nm) for nm in "AYQS"]
            pA = bkA[:].rearrange("p (a b) -> p a b", a=4)
            pYX = bkY[:].rearrange("p (t a b) -> p t a b", t=2, a=4)
            ptm = bkQ[:].rearrange("p (a b) -> p a b", a=4)
            pQ = bkQ[:, 0:256].rearrange("p (a b) -> p a b", a=4)
            pS = bkS[:].rearrange("p (t a b) -> p t a b", t=2, a=4)
            bptm = bpQ

            order = list(range(NRB)) if d == 0 else list(range(NRB - 1, -1, -1))
            corder = list(range(NCB)) if d == 0 else list(range(NCB - 1, -1, -1))

            def issue_loads(i):
                blk = order[i]
                sl = i % 2
                t0 = blk * RB
                c0 = t0 // CH
                for p in range(4):
                    S.dma("sp", ar_s[sl][:, p, :, :, :], ard[d, :, p, c0:c0 + NCB, :, :], bar_s[sl], writes=[bar_s[sl]])
                S.dma("sp", btr_s[sl][:], btd[d, :, :, t0:t0 + RB], bbtr_s[sl], writes=[bbtr_s[sl]])
                S.dma("sp", ktr_s[sl][:], ktd[d, :, :, t0:t0 + RB], bktr_s[sl], writes=[bktr_s[sl]])
                S.dma("sp", vr_s[sl][:], vrd[:, :, t0:t0 + RB], bvr_s[sl], writes=[bvr_s[sl]])
                S.dma("sp", gam_s[sl][:], gamd[d, :, :, c0:c0 + NCB], bgam_s[sl], writes=[bgam_s[sl]])

            issue_loads(0)
            yield

            def hloop():
                for h in HORD:
                    p, q0 = h // 2, (h % 2) * 64
                    yield h, p, slice(q0, q0 + 64)

            for i, blk in enumerate(order):
                bs = i % 2
                t0 = blk * RB
                ar, btr, ktr, vr, gam = ar_s[bs], btr_s[bs], ktr_s[bs], vr_s[bs], gam_s[bs]
                bar, bbtr, bktr, bvr, bgam = bar_s[bs], bbtr_s[bs], bktr_s[bs], bvr_s[bs], bgam_s[bs]
                oT_, boT_ = oT[bs], boT[bs]

                def prep_chunk(c):
                    sl = c % 2
                    cs = slice(c * CH, (c + 1) * CH)
                    for h, p, qs in hloop():
                        rhs = ar[qs, p, c, :, :].rearrange("k a t -> k (a t)")
                        S.mm(pA[qs, p, :], btr[qs, p, cs], rhs, reads=[bbtr, bar], writes=[bpA])
                    for h, p, qs in hloop():
                        S.mm(pYX[qs, 0, p, :], ar[qs, p, c, 0, :], btr[qs, p, cs], reads=[bbtr, bar], writes=[bpY])
                    yield
                    S.tt("dve", A1m[sl][:], pA, bc(mask2[:], 1, [128, 4, 128]), ALU.mult, reads=[bpA, bmk], writes=[bA1m[sl]])
                    S.tt("dve", Yk[sl][:], pYX[:, 0, :, :], bc(maskY[:], 1, [128, 4, 64]), ALU.mult, reads=[bpY, bmk], writes=[bYk[sl]])
                    for h, p, qs in hloop():
                        rhs = ar[qs, p, c, :, :].rearrange("k a t -> k (a t)")
                        S.mm(pA[qs, p, :], ktr[qs, p, cs], rhs, reads=[bktr, bar], writes=[bpA])
                    yield
                    S.tt("dve", A2m[sl][:], pA, bc(mask2[:], 1, [128, 4, 128]), ALU.mult, reads=[bpA, bmk], writes=[bA2m[sl]])
                    S.copy("act", Xk[sl][:], A1m[sl][:, :, 0:64], reads=[bA1m[sl]], writes=[bXk[sl]])
                    S.tt("pool", Qk[sl][:], A1m[sl][:, :, 0:64], bc(I2[:], 1, [128, 4, 64]), ALU.add, reads=[bA1m[sl], bcst], writes=[bQk[sl]])
                    yield
                    for lvl in range(1, 6):
                        last = lvl == 5
                        for h, p, qs in hloop():
                            if not last:
                                S.mm(pYX[qs, 1, p, :], Yk[sl][qs, p, :], Xk[sl][qs, p, :], reads=[bYk[sl], bXk[sl]], writes=[bpY])
                            S.mm(pYX[qs, 0, p, :], Xk[sl][qs, p, :], Yk[sl][qs, p, :], reads=[bYk[sl], bXk[sl]], writes=[bpY])
                        yield
                        S.tt("dve", IY[sl][:], pYX[:, 0, :, :], bc(I2[:], 1, [128, 4, 64]), ALU.add, reads=[bpY, bcst], writes=[bIY[sl]])
                        if not last:
                            S.copy("act", Xk[sl][:], pYX[:, 1, :, :], reads=[bpY], writes=[bXk[sl]])
                            S.copy("act", Yk[sl][:], pYX[:, 0, :, :], reads=[bpY], writes=[bYk[sl]])
                        yield
                        for h, p, qs in hloop():
                            S.mm(pQ[qs, p, :], IY[sl][qs, p, :], Qk[sl][qs, p, :], reads=[bIY[sl], bQk[sl]], writes=[bpQ])
                        yield
                        if last:
                            S.copy("act", TT[sl][:], pQ, reads=[bpQ], writes=[bTT[sl]])
                        else:
                            S.copy("act", Qk[sl][:], pQ, reads=[bpQ], writes=[bQk[sl]])
                        yield

                def seq_chunk(c):
                    sl = c % 2
                    for h, p, qs in hloop():
                        hc = slice(h * 64, (h + 1) * 64)
                        S.mm(pS[qs, 0, p, :], ar[qs, p, c, 0, :], Hr[qs, p, :], start=True, stop=False, reads=[bar, bH], writes=[bpS])
                        S.mm(pS[qs, 0, p, :], A2m[sl][qs, p, 0:64], tms[2][qs, c, hc], start=False, stop=True, reads=[bA2m[sl], btm[2]], writes=[bpS])
                    yield
                    S.copy("act", Xs[:], pS[:, 0, :, :], reads=[bpS], writes=[bXs])
                    yield
                    for h, p, qs in hloop():
                        S.mm(pS[qs, 1, p, :], TT[sl][qs, p, :], Xs[qs, p, :], reads=[bTT[sl], bXs], writes=[bpS])
                    yield
                    S.copy("act", Us[:], pS[:, 1, :, :], reads=[bpS], writes=[bUs])
                    yield
                    for h, p, qs in hloop():
                        hc = slice(h * 64, (h + 1) * 64)
                        S.mm(pS[qs, 0, p, :], Hr[qs, p, :], ar[qs, p, c, 1, :], start=True, stop=False, reads=[bH, bar], writes=[bpS])
                        S.mm(pS[qs, 0, p, :], Us[qs, p, :], A1m[sl][qs, p, 64:128], start=False, stop=False, reads=[bUs, bA1m[sl]], writes=[bpS])
                        S.mm(pS[qs, 0, p, :], tms[2][qs, c, hc], A2m[sl][qs, p, 64:128], start=False, stop=True, reads=[btm[2], bA2m[sl]], writes=[bpS])
                    for h, p, qs in hloop():
                        hc = slice(h * 64, (h + 1) * 64)
                        S.mm(pS[qs, 1, p, :], tms[0][qs, c, hc], Us[qs, p, :], start=True, stop=False, reads=[btm[0], bUs], writes=[bpS])
                        S.mm(pS[qs, 1, p, :], tms[1][qs, c, hc], tms[2][qs, c, hc], start=False, stop=True, reads=[btm[1], btm[2]], writes=[bpS])
                    yield
                    S.copy("act", oT_[:, :, c * CH:(c + 1) * CH], pS[:, 0, :, :], reads=[bpS], writes=[boT_])
                    S.tt("dve", Hst[:], Hst[:], pS[:, 1, :, :], ALU.add, reads=[bH, bpS], writes=[bH])
                    S.tt("dve", Hst[:], Hst[:], gam[:, :, c:c + 1].to_broadcast([128, 4, 64]), ALU.mult, reads=[bH, bgam], writes=[bH])
                    if RDT != F32:
                        S.copy("pool", Hr[:], Hst[:], reads=[bH], writes=[bH])
                    yield

                srcs = [(btr, bbtr), (ktr, bktr), (vr, bvr)]
                ncp = 0
                for ai, (src, bsrc) in enumerate(srcs):
                    for c in range(NCB):
                        for p in range(4):
                            for hf in range(2):
                                S.mm(ptm[hf * 64:(hf + 1) * 64, p, :], src[:, p, c * CH:(c + 1) * CH], identr[:, :],
                                     reads=[bsrc, bcst], writes=[bptm])
                        S.copy("act" if ncp % 2 == 0 else "dve", tms[ai][:, c, :], ptm.rearrange("p a b -> p (a b)"),
                               reads=[bptm], writes=[btm[ai]])
                        ncp += 1
                        yield
                if i + 1 < NRB:
                    issue_loads(i + 1)

                g = prep_chunk(corder[0])
                for _ in g:
                    yield
                for ci, c in enumerate(corder):
                    gens = [seq_chunk(c)]
                    if ci + 1 < NCB:
                        gens.append(prep_chunk(corder[ci + 1]))
                    while gens:
                        for g in list(gens):
                            try:
                                next(g)
                            except StopIteration:
                                gens.remove(g)
                        yield
                S.dma("sp", orawd[d, :, :, t0:t0 + RB], oT_[:], boT_, reads=[boT_])
                yield

        threads = [dir_thread(d) for d in dirs]
        while threads:
            for g in list(threads):
                try:
                    next(g)
                except StopIteration:
                    threads.remove(g)
        S.end()

    def phase_rpost():
        RB = 512
        S.begin()
        par, bpar = rw_params([ln_g, ln_b])
        bcst = S.buf("const")
        blk2 = make_blk2(bcst)
        sh4 = [128, 4, RB]

        def pcol(c):
            return par[:, c:c + 4].unsqueeze(2).to_broadcast(sh4)

        oT = [S.sb(f"oT{i}", [128, 4, RB], F32) for i in range(2)]
        boT = S.bufs_n(2, "oT")
        bon = [S.sb(f"bon{i}", [128, 4, RB], F32) for i in range(2)]
        bbon = S.bufs_n(2, "bon")
        gt = [S.sb(f"gt{i}", [128, 4, RB], BF16) for i in range(2)]
        bgt = S.bufs_n(2, "gt")
        sq = S.sb("sq", [128, 4, RB], F32)
        mean = S.sb("mean", [128, 4, RB], F32)
        ex2 = S.sb("ex2", [128, 4, RB], F32)
        bsq, bmean, bex2 = S.buf("sq"), S.buf("mean"), S.buf("ex2")
        oo = [S.sb(f"oo{i}", [128, 4, RB], BF16) for i in range(2)]
        boo = S.bufs_n(2, "oo")
        pp = [S.ps(f"pp{i}", [128, 512], F32) for i in range(4)]
        bpp = S.bufs_n(4, "ps_pp")
        n = 0
        k = 0
        for blk in range(NB):
            t0 = blk * RB
            S.dma("sp", bon[blk % 2][:], bond[:, :, t0:t0 + RB], bbon[blk % 2], writes=[bbon[blk % 2]])
            for d in range(2):
                o_, bo_ = oT[k % 2], boT[k % 2]
                g_, bg_ = gt[k % 2], bgt[k % 2]
                q_, bq_ = oo[k % 2], boo[k % 2]
                k += 1
                S.dma("sp", o_[:], orawd[d, :, :, t0:t0 + RB], bo_, writes=[bo_])
                S.dma("sp", g_[:], gtd[d, :, :, t0:t0 + RB], bg_, writes=[bg_])
                S.act(sq[:], o_[:], AF.Square, reads=[bo_], writes=[bsq])
                for p in range(4):
                    p1, b1 = pp[n % 4], bpp[n % 4]
                    n += 1
                    S.mm(p1[:], blk2[:], o_[:, p, :], reads=[bcst, bo_], writes=[b1])
                    S.act(mean[:, p, :], p1[:], AF.Identity, reads=[b1], writes=[bmean], scale=1.0 / 64)
                    p2, b2 = pp[n % 4], bpp[n % 4]
                    n += 1
                    S.mm(p2[:], blk2[:], sq[:, p, :], reads=[bcst, bsq], writes=[b2])
                    S.act(ex2[:, p, :], p2[:], AF.Identity, reads=[b2], writes=[bex2], scale=1.0 / 64)
                S.tt("pool", sq[:], mean[:], mean[:], ALU.mult, reads=[bmean], writes=[bsq])
                S.tt("dve", ex2[:], ex2[:], sq[:], ALU.subtract, reads=[bex2, bsq], writes=[bex2])
                S.act(ex2[:], ex2[:], AF.Sqrt, reads=[bex2], writes=[bex2], bias=GN_EPS)
                S.recip(ex2[:], ex2[:], reads=[bex2], writes=[bex2])
                S.tt("dve", o_[:], o_[:], mean[:], ALU.subtract, reads=[bo_, bmean], writes=[bo_])
                S.tt("dve", o_[:], o_[:], ex2[:], ALU.mult, reads=[bo_, bex2], writes=[bo_])
                S.tt("pool", o_[:], o_[:], pcol(0), ALU.mult, reads=[bo_, bpar], writes=[bo_])
                S.tt("pool", o_[:], o_[:], pcol(4), ALU.add, reads=[bo_, bpar], writes=[bo_])
                S.tt("dve", o_[:], o_[:], bon[blk % 2][:], ALU.add, reads=[bo_, bbon[blk % 2]], writes=[bo_])
                S.tt("dve", q_[:], o_[:], g_[:], ALU.mult, reads=[bo_, bg_], writes=[bq_])
                dst = ofT_v if d == 0 else obT_v
                S.dma("sp", dst[:, :, t0:t0 + RB], q_[:], bq_, reads=[bq_])
        S.end()

    phase_rprep()
    if stop_after == "RP":
        gst.close()
        return nc
    phase_rscan()
    if stop_after == "RS":
        gst.close()
        return nc
    phase_rpost()
    if stop_after in ("R0", "R1"):
        gst.close()
        return nc

    def phase_merge():
        S.begin()
        wo_a = S.sb("wo_a", [64, 8, D], BF16)
        wo_b = S.sb("wo_b", [128, 4, D], BF16)
        wm = S.sb("wm", [128, 8, D], BF16)
        bwt = S.buf("wts")
        wst = [S.sb(f"wst{i}", [128, D], F32) for i in range(2)]
        bwst = S.bufs_n(2, "wst")
        n = 0
        for h in range(8):
            sl = n % 2
            S.dma("sp", wst[sl][0:64, :], w_o[h * 64:(h + 1) * 64, :], bwst[sl], writes=[bwst[sl]])
            S.copy(["dve", "pool"][n % 2], wo_a[:, h, :], wst[sl][0:64, :], reads=[bwst[sl]], writes=[bwt])
            n += 1
        for p in range(4):
            sl = n % 2
            S.dma("sp", wst[sl][:], w_o[512 + p * 128:512 + (p + 1) * 128, :], bwst[sl], writes=[bwst[sl]])
            S.copy(["dve", "pool"][n % 2], wo_b[:, p, :], wst[sl][:], reads=[bwst[sl]], writes=[bwt])
            n += 1
        for kc in range(8):
            sl = n % 2
            S.dma("sp", wst[sl][:], w_merge[kc * 128:(kc + 1) * 128, :], bwst[sl], writes=[bwst[sl]])
            S.copy(["dve", "pool"][n % 2], wm[:, kc, :], wst[sl][:], reads=[bwst[sl]], writes=[bwt])
            n += 1
        g2_bc = S.sb("g2_bc", [128, D], F32)
        bg = S.buf("g")
        S.dma("sp", g2_bc[:], norm_ffn_g.partition_broadcast(128), bg, writes=[bg])
        oa_sb = S.sb("oa_sb", [64, 8, 512], BF16)
        ob_sb = S.sb("ob_sb", [128, 4, 512], BF16)
        of_sb = S.sb("of_sb", [128, 4, 512], BF16)
        bof = S.buf("of_in")
        ofT_v = ofT.rearrange("a p s -> p a s")
        gt_sb = S.sb("gt_sb", [128, 16, 512], BF16)
        boa, bob, bgt = S.bufs_n(3, "in")
        z = S.sb("z", [128, 8, 512], BF16)
        bz = S.buf("z")
        t1 = [S.sb(f"t1_{i}", [128, 512], F32) for i in range(2)]
        t2 = [S.sb(f"t2_{i}", [128, 512], F32) for i in range(2)]
        bt1 = S.bufs_n(2, "t1")
        bt2 = S.bufs_n(2, "t2")
        pa = [S.ps(f"pa{i}", [128, 512], F32) for i in range(2)]
        pbk = [S.ps(f"pbk{i}", [128, 512], F32) for i in range(2)]
        bpa = S.bufs_n(2, "ps_a")
        bpbk = S.bufs_n(2, "ps_b")
        pmg = [S.ps(f"pmg{i}", [128, 512], F32) for i in range(2)]
        bpmg = S.bufs_n(2, "ps_mg")
        pT2 = S.ps("pT2", [128, 8, 128], BF16)
        bpT2 = S.buf("ps_T2")
        xt = [S.sb(f"xt{i}", [128, D], F32) for i in range(2)]
        bxt = S.bufs_n(2, "xt")
        x1t = [S.sb(f"x1t{i}", [128, D], F32) for i in range(2)]
        bx1t = S.bufs_n(2, "x1t")
        junk = S.sb("junk", [128, D], BF16)
        bjunk = S.buf("junk")
        st = S.sb("st", [128, 2], F32)
        bst = S.buf("st")
        h2n = S.sb("h2n", [128, D], BF16)
        bh2n = S.buf("h2n")
        h2t = [S.sb(f"h2t{i}", [128, 8, 128], BF16) for i in range(2)]
        bh2t = S.bufs_n(2, "h2t")
        oaT_v = oaT.rearrange("h d s -> d h s")
        obT_v = obT.rearrange("a p s -> p a s")
        gat_v = gat.rearrange("j p s -> p j s")
        h2T_v = h2T.rearrange("kc p s -> p kc s")
        nmg = 0
        for blk in range(NB):
            cs = slice(blk * 512, (blk + 1) * 512)
            S.dma("sp", oa_sb[:], oaT_v[:, :, cs], boa, writes=[boa])
            S.dma("sp", ob_sb[:], obT_v[:, :, cs], bob, writes=[bob])
            S.dma("sp", of_sb[:], ofT_v[:, :, cs], bof, writes=[bof])
            S.dma("sp", gt_sb[:, 0:8, :], gat_v[:, 0:8, cs], bgt, writes=[bgt])
            S.dma("sp", gt_sb[:, 8:16, :], gat_v[:, 8:16, cs], bgt, writes=[bgt])
            for fc in range(8):
                sl = fc % 2
                fs = slice(fc * 128, (fc + 1) * 128)
                for h in range(8):
                    S.mm(pa[sl][:], wo_a[:, h, fs], oa_sb[:, h, :], start=(h == 0), stop=(h == 7),
                         reads=[bwt, boa], writes=[bpa[sl]])
                for p in range(4):
                    S.mm(pbk[sl][:], wo_b[:, p, fs], of_sb[:, p, :], start=(p == 0), stop=False,
                         reads=[bwt, bof], writes=[bpbk[sl]])
                for p in range(4):
                    S.mm(pbk[sl][:], wo_b[:, p, fs], ob_sb[:, p, :], start=False, stop=(p == 3),
                         reads=[bwt, bob], writes=[bpbk[sl]])
                S.tt("dve", t1[sl][:], pa[sl][:], gt_sb[:, fc, :], ALU.mult, reads=[bpa[sl], bgt], writes=[bt1[sl]])
                S.tt("dve", t2[sl][:], pbk[sl][:], gt_sb[:, 8 + fc, :], ALU.mult, reads=[bpbk[sl], bgt], writes=[bt2[sl]])
                S.tt("pool", z[:, fc, :], t1[sl][:], t2[sl][:], ALU.add, reads=[bt1[sl], bt2[sl]], writes=[bz])
            for tl in range(4):
                t = blk * 4 + tl
                xb, bxb = xt[t % 2], bxt[t % 2]
                x1b, bx1b = x1t[t % 2], bx1t[t % 2]
                S.dma("sp", xb[:], x[t * 128:(t + 1) * 128, :], bxb, writes=[bxb])
                for hf in range(2):
                    pm_, bpm_ = pmg[nmg % 2], bpmg[nmg % 2]
                    nmg += 1
                    for kc in range(8):
                        S.mm(pm_[:], z[:, kc, tl * 128:(tl + 1) * 128], wm[:, kc, hf * 512:(hf + 1) * 512],
                             start=(kc == 0), stop=(kc == 7), reads=[bz, bwt], writes=[bpm_])
                    S.tt("dve", x1b[:, hf * 512:(hf + 1) * 512], pm_[:], xb[:, hf * 512:(hf + 1) * 512], ALU.add,
                         reads=[bpm_, bxb], writes=[bx1b])
                S.dma("sp", x1[t * 128:(t + 1) * 128, :], x1b[:], bx1b, reads=[bx1b])
                S.act(junk[:], x1b[:], AF.Square, reads=[bx1b], writes=[bjunk, bst], accum_out=st[:, 0:1])
                S.act(st[:, 0:1], st[:, 0:1], AF.Sqrt, reads=[bst], writes=[bst], scale=1.0 / D, bias=NORM_EPS)
                S.recip(st[:, 0:1], st[:, 0:1], reads=[bst], writes=[bst])
                S.stt(h2n[:], x1b[:], st[:, 0:1], g2_bc[:], ALU.mult, ALU.mult, reads=[bx1b, bst, bg], writes=[bh2n])
                for kc in range(8):
                    S.tr(pT2[:, kc, :], h2n[:, kc * 128:(kc + 1) * 128], identb[:], reads=[bh2n], writes=[bpT2])
                ho, bho = h2t[t % 2], bh2t[t % 2]
                S.copy("act", ho[:], pT2[:], reads=[bpT2], writes=[bho])
                S.dma("sp", h2T_v[:, :, t * 128:(t + 1) * 128], ho[:], bho, reads=[bho])
        S.end()

    phase_merge()
    if stop_after == "M":
        gst.close()
        return nc

    def phase_f1():
        S.begin()
        h2_sb = S.sb("h2_sb", [128, 8, S_LEN], BF16)
        bh2 = S.buf("h2")
        S.dma("sp", h2_sb[:], h2T.rearrange("kc p s -> p kc s"), bh2, writes=[bh2])
        cst = S.sb("cst", [4, FFN], F32)
        bcst = S.buf("cst")
        S.dma("sp", cst[0:3, :], conv_w[:, :], bcst, writes=[bcst])
        S.dma("sp", cst[3:4, :], conv_b.rearrange("(o n) -> o n", o=1), bcst, writes=[bcst])
        pcw = S.ps("pcw", [128, NHC, 4], F32)
        bpcw = S.buf("ps_cw")
        cw = S.sb("cw", [128, NHC, 4], F32)
        bcw = S.buf("cw")
        for hc in range(NHC):
            S.tr(pcw[:, hc, :], cst[0:4, hc * 128:(hc + 1) * 128], ident[0:4, 0:4], reads=[bcst], writes=[bpcw])
        S.copy("dve", cw[:], pcw[:], reads=[bpcw], writes=[bcw])
        wgst = [S.sb(f"wgst{i}", [128, 8, 128], F32) for i in range(2)]
        wust = [S.sb(f"wust{i}", [128, 8, 128], F32) for i in range(2)]
        wg = [S.sb(f"wg{i}", [128, 8, 128], BF16) for i in range(2)]
        wu = [S.sb(f"wu{i}", [128, 8, 128], BF16) for i in range(2)]
        bwgst, bwust, bwg, bwu = [S.bufs_n(2, nm) for nm in ("wgst", "wust", "wg", "wu")]
        G_sb = S.sb("G_sb", [128, S_LEN + 2], F32)
        bG = S.buf("G")
        S.memset("pool", G_sb[:, 0:1], 0.0, writes=[bG])
        S.memset("pool", G_sb[:, S_LEN + 1:S_LEN + 2], 0.0, writes=[bG])
        gp = S.sb("gp", [128, S_LEN], F32)
        bgp = S.buf("gp")
        sg = S.sb("sg", [128, S_LEN], F32)
        bsg = S.buf("sg")
        at_sb = [S.sb(f"at{i}", [128, 512], BF16) for i in range(3)]
        bat = S.bufs_n(3, "at")
        pg = [S.ps(f"pg{i}", [128, 512], F32) for i in range(2)]
        bpg = S.bufs_n(2, "ps_g")
        pu = [S.ps(f"pu{i}", [128, 512], F32) for i in range(2)]
        bpu = S.bufs_n(2, "ps_u")
        wg_v = w_gate.rearrange("(kc p) n -> p kc n", p=128)
        wu_v = w_up.rearrange("(kc p) n -> p kc n", p=128)
        nat = [0]
        gp2 = [gp, S.sb("gp_b", [128, S_LEN], F32)]
        sg2 = [sg, S.sb("sg_b", [128, S_LEN], F32)]
        bgp2 = [bgp, S.buf("gp_b")]
        bsg2 = [bsg, S.buf("sg_b")]

        def gate_part(hc):
            sl = hc % 2
            hs = slice(hc * 128, (hc + 1) * 128)
            gp_, bgp_, sg_, bsg_ = gp2[sl], bgp2[sl], sg2[sl], bsg2[sl]
            S.dma("sp", wgst[sl][:], wg_v[:, :, hs], bwgst[sl], writes=[bwgst[sl]])
            S.dma("sp", wust[sl][:], wu_v[:, :, hs], bwust[sl], writes=[bwust[sl]])
            S.copy("pool", wg[sl][:], wgst[sl][:], reads=[bwgst[sl]], writes=[bwg[sl]])
            S.copy("pool", wu[sl][:], wust[sl][:], reads=[bwust[sl]], writes=[bwu[sl]])
            for tb in range(NB):
                p_, bp_ = pg[tb % 2], bpg[tb % 2]
                for kc in range(8):
                    S.mm(p_[:], wg[sl][:, kc, :], h2_sb[:, kc, tb * 512:(tb + 1) * 512], start=(kc == 0), stop=(kc == 7),
                         reads=[bwg[sl], bh2], writes=[bp_])
                S.copy("act", G_sb[:, 1 + tb * 512:1 + (tb + 1) * 512], p_[:], reads=[bp_], writes=[bG])
            S.ts("pool", gp_[:], G_sb[:, 1:S_LEN + 1], cw[:, hc, 1:2], cw[:, hc, 3:4], ALU.mult, ALU.add,
                 reads=[bG, bcw], writes=[bgp_])
            S.stt(gp_[:], G_sb[:, 0:S_LEN], cw[:, hc, 0:1], gp_[:], ALU.mult, ALU.add, reads=[bG, bcw, bgp_], writes=[bgp_])
            S.stt(gp_[:], G_sb[:, 2:S_LEN + 2], cw[:, hc, 2:3], gp_[:], ALU.mult, ALU.add, reads=[bG, bcw, bgp_], writes=[bgp_])
            S.act(sg_[:], gp_[:], AF.Silu, reads=[bgp_], writes=[bsg_])

        def up_part(hc):
            sl = hc % 2
            sg_, bsg_ = sg2[sl], bsg2[sl]
            for tb in range(NB):
                p_, bp_ = pu[tb % 2], bpu[tb % 2]
                for kc in range(8):
                    S.mm(p_[:], wu[sl][:, kc, :], h2_sb[:, kc, tb * 512:(tb + 1) * 512], start=(kc == 0), stop=(kc == 7),
                         reads=[bwu[sl], bh2], writes=[bp_])
                a_, ba_ = at_sb[nat[0] % 3], bat[nat[0] % 3]
                nat[0] += 1
                S.tt("dve", a_[:], p_[:], sg_[:, tb * 512:(tb + 1) * 512], ALU.mult, reads=[bp_, bsg_], writes=[ba_])
                S.dma("sp", actT[tb * 4:(tb + 1) * 4, :, hc, :].rearrange("t p s -> p t s"),
                      a_[:].rearrange("p (t s) -> p t s", s=128), ba_, reads=[ba_])

        for hc in range(NHC):
            gate_part(hc)
            if hc >= 1:
                up_part(hc - 1)
        up_part(NHC - 1)
        S.end()

    phase_f1()
    if stop_after == "F1":
        gst.close()
        return nc

    def phase_f2():
        S.begin()
        wd = S.sb("wd", [128, NHC, D], BF16)
        bwd = S.buf("wd")
        wst = [S.sb(f"wst{i}", [128, D], F32) for i in range(2)]
        bwst = S.bufs_n(2, "wst")
        for hc in range(NHC):
            sl = hc % 2
            S.dma("sp", wst[sl][:], w_down[hc * 128:(hc + 1) * 128, :], bwst[sl], writes=[bwst[sl]])
            S.copy(["dve", "pool"][hc % 2], wd[:, hc, :], wst[sl][:], reads=[bwst[sl]], writes=[bwd])
        at = [S.sb(f"at{i}", [128, NHC, 128], BF16) for i in range(2)]
        bat = S.bufs_n(2, "at")
        x1t = [S.sb(f"x1t{i}", [128, D], F32) for i in range(2)]
        bx1t = S.bufs_n(2, "x1t")
        yt = [S.sb(f"yt{i}", [128, D], F32) for i in range(2)]
        byt = S.bufs_n(2, "yt")
        pd = [S.ps(f"pd{i}", [128, 512], F32) for i in range(2)]
        bpd = S.bufs_n(2, "ps_d")
        n = 0
        for t in range(NT):
            a_, ba_ = at[t % 2], bat[t % 2]
            xb, bxb = x1t[t % 2], bx1t[t % 2]
            yb, byb = yt[t % 2], byt[t % 2]
            S.dma("sp", a_[:], actT[t, :, :, :], ba_, writes=[ba_])
            S.dma("sp", xb[:], x1[t * 128:(t + 1) * 128, :], bxb, writes=[bxb])
            for hf in range(2):
                p_, bp_ = pd[n % 2], bpd[n % 2]
                n += 1
                for hc in range(NHC):
                    S.mm(p_[:], a_[:, hc, :], wd[:, hc, hf * 512:(hf + 1) * 512], start=(hc == 0), stop=(hc == NHC - 1),
                         reads=[ba_, bwd], writes=[bp_])
                S.tt("dve", yb[:, hf * 512:(hf + 1) * 512], p_[:], xb[:, hf * 512:(hf + 1) * 512], ALU.add,
                     reads=[bp_, bxb], writes=[byb])
            S.dma("sp", y[t * 128:(t + 1) * 128, :], yb[:], byb, reads=[byb])
        S.end()

    phase_f2()

    gst.close()
    return nc


INPUT_NAMES = ["x", "positions", "norm_mix_g", "w_in", "b_gate", "q_a_norm_g", "kv_a_norm_g",
               "w_uq", "w_ukv", "qn_norm_g", "qr_norm_g", "kn_norm_g", "kr_norm_g", "shift_mu",
               "w0", "w2", "a0", "a2", "g2", "k_k", "k_a", "r_k", "ln_x_g", "ln_x_b", "w_o",
               "w_merge", "norm_ffn_g", "w_ffn_gate", "w_ffn_up", "ffn_conv_w", "ffn_conv_b",
               "w_ffn_down"]


def make_in_maps(inputs, n_cores, S_LEN):
    inv_freq = (10000.0 ** (-np.arange(0, ROPE, 2, dtype=np.float32) / ROPE)).astype(np.float32)
    maps = []
    for c in range(n_cores):
        m = {"inv_freq": inv_freq}
        for k in INPUT_NAMES:
            a = np.asarray(inputs[k])
            if k == "x":
                m[k] = np.ascontiguousarray(a[c, :S_LEN])
            elif k == "positions":
                m[k] = np.ascontiguousarray(a[c, :S_LEN]).astype(np.int32)
            else:
                a = a[0]
                if k == "r_k":
                    a = a.reshape(512)
                m[k] = np.ascontiguousarray(a).astype(np.float32)
        maps.append(m)
    return maps


_NC_CACHE = {}


def kernel(**inputs):
    S_LEN = 4096
    B = 8
    if "nc" not in _NC_CACHE:
        _NC_CACHE["nc"] = build(S_LEN)
    nc = _NC_CACHE["nc"]
    maps = make_in_maps(inputs, B, S_LEN)
    res = run_bass_kernel_spmd(nc, maps, core_ids=list(range(B)))
    out = np.stack([np.asarray(r["y"]) for r in res.results], axis=0)
    return out.astype(np.float32)
```

```python
import math
import numpy as np
from contextlib import ExitStack
import concourse.bass as bass
import concourse.mybir as mybir
from concourse.bass_utils import run_bass_kernel_spmd

F32 = mybir.dt.float32
BF16 = mybir.dt.bfloat16
I32 = mybir.dt.int32
AF = mybir.ActivationFunctionType
ALU = mybir.AluOpType
AX = mybir.AxisListType
ENGS = ["pe", "dve", "act", "pool", "sp"]

D = 1024
HEADS = 8
NOPE, ROPE, VH = 64, 32, 64
QLORA, KVLORA = 256, 128
MLA_COLS = 416
RW0 = 416
RW_COLS = 1984
G0 = 2400
IN_COLS = 4448
FFN = 2816
NHC = 22
NORM_EPS = 1e-6
GN_EPS = 64e-5
CH = 64
HORD = [0, 2, 4, 6, 1, 3, 5, 7]
PE_FENCE = False


class Buf:
    __slots__ = ("name", "w", "r", "dsem", "dcnt", "excl")

    def __init__(self, name, excl=False):
        self.name = name
        self.excl = excl
        self.w = None
        self.r = []
        self.dsem = None
        self.dcnt = 0


class Sched:
    NDPOOL = 64

    def __init__(self, nc, gstack):
        self.nc = nc
        self.bufs = []
        self.phase = 0
        self.stack = None
        self.esem = {e: gstack.enter_context(nc.semaphore(f"se_{e}")) for e in ENGS}
        self.dpool = [gstack.enter_context(nc.semaphore(f"sd_{i}")) for i in range(self.NDPOOL)]

    def buf(self, name="b"):
        b = Buf(name, excl=name.startswith("ps"))
        self.bufs.append(b)
        return b

    def bufs_n(self, n, name="b"):
        return [self.buf(f"{name}{i}") for i in range(n)]

    def begin(self):
        self.phase += 1
        self.stack = ExitStack()
        self.prog = {e: [] for e in ENGS}
        self.cnt = {e: 0 for e in ENGS}
        self.sem = dict(self.esem)
        self.waited = {e: {} for e in ENGS}
        self.dbufs = []
        for b in self.bufs:
            b.w = None
            b.r = []
            b.dsem = None
            b.dcnt = 0
        return self.stack

    def sb(self, name, shape, dt):
        return self.stack.enter_context(
            self.nc.sbuf_tensor(f"{name}_{self.phase}", shape, dt))

    def ps(self, name, shape, dt):
        return self.stack.enter_context(
            self.nc.psum_tensor(f"{name}_{self.phase}", shape, dt))

    def _waits(self, eng, reads, writes):
        toks = []
        for b in reads:
            if b.w is not None:
                toks.append(b.w)
            if b.excl:
                toks.extend(t for t in b.r if not (t[0] == "e" and t[1] == eng))
        for b in writes:
            if b.w is not None:
                toks.append(b.w)
            toks.extend(b.r)
        waits = []
        for t in toks:
            if t[0] == "e":
                _, e2, val = t
                if e2 == eng and eng == "pe":
                    continue
                key = ("e", e2)
                sem = self.sem[e2]
            else:
                b = t[1]
                key = ("d", id(b))
                sem = b.dsem
                val = b.dcnt
            if self.waited[eng].get(key, 0) >= val:
                continue
            self.waited[eng][key] = val
            waits.append((sem, val))
        return waits

    def op(self, eng, fn, reads=(), writes=()):
        waits = self._waits(eng, reads, writes)
        if eng == "pe" and getattr(self, "fence", False):
            self.fence = False
            if self.cnt["pe"] > 0:
                waits.append((self.sem["pe"], self.cnt["pe"]))
        self.cnt[eng] += 1
        tok = ("e", eng, self.cnt[eng])
        self.prog[eng].append((waits, fn, (self.sem[eng], 1)))
        for b in reads:
            b.r.append(tok)
        for b in writes:
            b.w = tok
            b.r = []

    def pe_fence(self):
        self.fence = True

    def dma(self, eng, out, in_, sbuf, reads=(), writes=(), **kw):
        waits = self._waits(eng, reads, writes)
        if sbuf.dsem is None:
            assert len(self.dbufs) < self.NDPOOL, "out of DMA semaphores"
            sbuf.dsem = self.dpool[len(self.dbufs)]
            self.dbufs.append(sbuf)
        sbuf.dcnt += 16
        tok = ("d", sbuf)
        self.prog[eng].append(
            (waits, lambda e: e.dma_start(out=out, in_=in_, **kw), (sbuf.dsem, 16)))
        for b in reads:
            b.r.append(tok)
        for b in writes:
            b.w = tok
            b.r = []

    def end(self):
        waits = []
        for b in self.dbufs:
            if self.waited["sp"].get(("d", id(b)), 0) < b.dcnt:
                waits.append((b.dsem, b.dcnt))
        self.prog["sp"].append((waits, None, None))
        prog = self.prog

        def mk(e):
            def f(engine):
                for waits, fn, inc in prog[e]:
                    for sem, val in waits:
                        engine.wait_ge(sem, val)
                    if fn is not None:
                        fn(engine).then_inc(inc[0], inc[1])
            return f

        allsems = [self.sem[e] for e in ENGS] + [b.dsem for b in self.dbufs]

        def clr(engine):
            for sm in allsems:
                engine.sem_clear(sm)

        def nop_(engine):
            pass

        with self.nc.Block() as block0:
            block0.tensor(nop_)
            block0.vector(nop_)
            block0.scalar(nop_)
            block0.gpsimd(nop_)
            block0.sync(clr)
        with self.nc.Block() as block:
            block.tensor(mk("pe"))
            block.vector(mk("dve"))
            block.scalar(mk("act"))
            block.gpsimd(mk("pool"))
            block.sync(mk("sp"))
        self.stack.close()
        self.stack = None

    def mm(self, out, lhsT, rhs, start=True, stop=True, reads=(), writes=()):
        self.op("pe", lambda e: e.matmul(out, lhsT, rhs, start=start, stop=stop),
                reads, writes)

    def tr(self, out, in_, ident, reads=(), writes=()):
        self.op("pe", lambda e: e.transpose(out, in_, ident), reads, writes)

    def act(self, out, in_, func, reads=(), writes=(), **kw):
        self.op("act", lambda e: e.activation(out, in_, func, **kw), reads, writes)

    def copy(self, eng, out, in_, reads=(), writes=()):
        if eng == "act":
            self.op(eng, lambda e: e.copy(out, in_), reads, writes)
        else:
            self.op(eng, lambda e: e.tensor_copy(out, in_), reads, writes)

    def tt(self, eng, out, in0, in1, op, reads=(), writes=()):
        self.op(eng, lambda e: e.tensor_tensor(out, in0, in1, op), reads, writes)

    def ts(self, eng, out, in0, s1, s2, op0, op1=None, reads=(), writes=()):
        if op1 is None:
            self.op(eng, lambda e: e.tensor_scalar(out, in0, s1, None, op0), reads, writes)
        else:
            self.op(eng, lambda e: e.tensor_scalar(out, in0, s1, s2, op0, op1), reads, writes)

    def stt(self, out, in0, scalar, in1, op0, op1, reads=(), writes=()):
        self.op("dve", lambda e: e.scalar_tensor_tensor(out, in0, scalar, in1, op0, op1),
                reads, writes)

    def red(self, out, in_, reads=(), writes=()):
        self.op("dve", lambda e: e.tensor_reduce(out, in_, AX.X, ALU.add), reads, writes)

    def recip(self, out, in_, reads=(), writes=()):
        self.op("dve", lambda e: e.reciprocal(out, in_), reads, writes)

    def memset(self, eng, ap, val, writes=()):
        self.op(eng, lambda e: e.memset(ap, val), (), writes)


def bc(ap, axis, shape):
    return ap.unsqueeze(axis).to_broadcast(shape)


def build(S_LEN=4096, dbg=False, stop_after=None, rbf16=True, ibf16=True):
    NT = S_LEN // 128
    NB = S_LEN // 512
    NCK = S_LEN // CH
    nc = bass.Bass("TRN2", target_bir_lowering=False)
    okind = "ExternalOutput" if dbg else "Internal"

    def din(name, shape, dt=F32):
        return nc.dram_tensor(name, shape, dt, kind="ExternalInput").ap()

    def dscr(name, shape, dt=F32):
        return nc.dram_tensor(name, shape, dt, kind=okind).ap()

    x = din("x", [S_LEN, D])
    pos = din("positions", [S_LEN], I32)
    invf = din("inv_freq", [16])
    norm_mix_g = din("norm_mix_g", [D])
    w_in = din("w_in", [D, IN_COLS])
    b_gate = din("b_gate", [2, D])
    q_a_norm_g = din("q_a_norm_g", [QLORA])
    kv_a_norm_g = din("kv_a_norm_g", [KVLORA])
    w_uq = din("w_uq", [QLORA, 768])
    w_ukv = din("w_ukv", [KVLORA, 1024])
    qn_g = din("qn_norm_g", [NOPE])
    qr_g = din("qr_norm_g", [ROPE])
    kn_g = din("kn_norm_g", [NOPE])
    kr_g = din("kr_norm_g", [ROPE])
    shift_mu = din("shift_mu", [2, RW_COLS])
    w0 = din("w0", [2, 512])
    w2 = din("w2", [2, 64, 512])
    a0 = din("a0", [512])
    a2 = din("a2", [64, 512])
    g2 = din("g2", [2, 128, 512])
    k_k = din("k_k", [512])
    k_a = din("k_a", [512])
    r_k = din("r_k", [512])
    ln_g = din("ln_x_g", [512])
    ln_b = din("ln_x_b", [512])
    w_o = din("w_o", [D, D])
    w_merge = din("w_merge", [D, D])
    norm_ffn_g = din("norm_ffn_g", [D])
    w_gate = din("w_ffn_gate", [D, FFN])
    w_up = din("w_ffn_up", [D, FFN])
    conv_w = din("ffn_conv_w", [3, FFN])
    conv_b = din("ffn_conv_b", [FFN])
    w_down = din("w_ffn_down", [FFN, D])
    y = nc.dram_tensor("y", [S_LEN, D], F32, kind="ExternalOutput").ap()

    prw = dscr("prw", [16, 128, S_LEN])
    gat = dscr("gat", [16, 128, S_LEN], BF16)
    qT = dscr("qT", [HEADS, 96, S_LEN], BF16)
    kT = dscr("kT", [HEADS, 96, S_LEN], BF16)
    vaug = dscr("vaug", [S_LEN, HEADS * 65], BF16)
    oaT = dscr("oaT", [HEADS, 64, S_LEN], BF16)
    ofT = dscr("ofT", [4, 128, S_LEN], BF16)
    obT = dscr("obT", [4, 128, S_LEN], BF16)
    x1 = dscr("x1", [S_LEN, D])
    h2T = dscr("h2T", [8, 128, S_LEN], BF16)
    actT = dscr("actT", [NT, 128, NHC, 128], BF16)

    gst = ExitStack()
    S = Sched(nc, gst)

    def gsb(name, shape, dt):
        return gst.enter_context(nc.sbuf_tensor(name, shape, dt))

    ident = gsb("ident", [128, 128], F32)
    identb = gsb("identb", [128, 128], BF16)
    ones = gsb("ones", [128, 128], F32)

    def load_col(eng, dst, src_vec, b, n=128):
        S.dma(eng, dst, src_vec.rearrange("(p o) -> p o", o=1), b, writes=[b])

    S.begin()
    b0 = S.buf()
    S.memset("pool", ones[:], 1.0, writes=[b0])
    S.op("pool", lambda e: e.affine_select(ident[:], ones[:], [[1, 128]], ALU.is_equal, 0.0,
                                           base=0, channel_multiplier=-1),
         reads=[b0], writes=[b0])
    S.copy("dve", identb[:], ident[:], reads=[b0], writes=[b0])
    S.end()

    def phase_a():
        S.begin()
        win = S.sb("win", [128, 8, IN_COLS], BF16)
        NG = 32
        GW = IN_COLS // NG
        wst = [S.sb(f"wst{i}", [128, 8, GW], F32) for i in range(2)]
        bwst = S.bufs_n(2, "wst")
        bwin = S.buf("win")
        w_in_v = w_in.rearrange("(kc p) n -> p kc n", p=128)
        for gi in range(NG):
            sl = gi % 2
            S.dma("sp", wst[sl][:], w_in_v[:, :, gi * GW:(gi + 1) * GW], bwst[sl], writes=[bwst[sl]])
            eng = ["dve", "pool"][gi % 2]
            S.copy(eng, win[:, :, gi * GW:(gi + 1) * GW], wst[sl][:], reads=[bwst[sl]], writes=[bwin])
        bsm = S.buf("small")
        wuq_st = S.sb("wuq_st", [128, 2, 768], F32)
        wuq = S.sb("wuq", [128, 2, 768], BF16)
        S.dma("sp", wuq_st[:], w_uq.rearrange("(kc p) n -> p kc n", p=128), bsm, writes=[bsm])
        S.copy("dve", wuq[:], wuq_st[:], reads=[bsm], writes=[bsm])
        wukv_st = S.sb("wukv_st", [128, 2, 8, 64], F32)
        wukv = S.sb("wukv", [128, 2, 8, 64], BF16)
        bsm2 = S.buf("small2")
        wukv_v = w_ukv.rearrange("k (h t d) -> k t h d", h=8, t=2, d=64)
        for tt_ in range(2):
            S.dma("sp", wukv_st[:, tt_, :, :], wukv_v[:, tt_, :, :], bsm2, writes=[bsm2])
        S.copy("dve", wukv[:], wukv_st[:], reads=[bsm2], writes=[bsm2])
        g_bc = S.sb("g_bc", [128, D], F32)
        gq_bc = S.sb("gq_bc", [128, QLORA], F32)
        gkv_bc = S.sb("gkv_bc", [128, KVLORA], F32)
        gkr_bc = S.sb("gkr_bc", [128, ROPE], F32)
        gqh_bc = S.sb("gqh_bc", [128, 96], F32)
        gkn_bc = S.sb("gkn_bc", [128, NOPE], F32)
        bg = S.buf("gains")
        S.dma("sp", g_bc[:], norm_mix_g.partition_broadcast(128), bg, writes=[bg])
        S.dma("sp", gq_bc[:], q_a_norm_g.partition_broadcast(128), bg, writes=[bg])
        S.dma("sp", gkv_bc[:], kv_a_norm_g.partition_broadcast(128), bg, writes=[bg])
        S.dma("sp", gkr_bc[:], kr_g.partition_broadcast(128), bg, writes=[bg])
        S.dma("sp", gqh_bc[:, 0:64], qn_g.partition_broadcast(128), bg, writes=[bg])
        S.dma("sp", gqh_bc[:, 64:96], qr_g.partition_broadcast(128), bg, writes=[bg])
        S.dma("sp", gkn_bc[:], kn_g.partition_broadcast(128), bg, writes=[bg])
        invf_bc = S.sb("invf_bc", [128, 16], F32)
        S.dma("sp", invf_bc[:], invf.partition_broadcast(128), bg, writes=[bg])
        bgate = S.sb("bgate", [128, 16], F32)
        bgf = b_gate.rearrange("a d -> (a d)")
        for j in range(16):
            load_col("sp", bgate[:, j:j + 1], bgf[j * 128:(j + 1) * 128], bg)
        posi = S.sb("posi", [128, NT], I32)
        posf = S.sb("posf", [128, NT], F32)
        bpos = S.buf("pos")
        for t in range(NT):
            load_col("sp", posi[:, t:t + 1], pos[t * 128:(t + 1) * 128], bpos)
        S.copy("dve", posf[:], posi[:], reads=[bpos], writes=[bpos])
        ang = S.sb("ang", [128, NT, 16], F32)
        kf = S.sb("kf", [128, NT, 16], F32)
        ki = S.sb("ki", [128, NT, 16], I32)
        rr = S.sb("rr", [128, NT, 16], F32)
        rc = ang
        mk = kf
        cos2 = S.sb("cos2", [128, NT, 32], F32)
        sin2 = S.sb("sin2", [128, NT, 32], F32)
        brp = S.buf("rope")
        TWO_PI = 2.0 * math.pi
        C1 = 6.28125
        C2 = TWO_PI - C1
        sh3 = [128, NT, 16]
        S.tt("dve", ang[:], bc(posf[:], 2, sh3), bc(invf_bc[:], 1, sh3), ALU.mult, reads=[bpos, bg], writes=[brp])
        S.ts("dve", kf[:], ang[:], 1.0 / TWO_PI, None, ALU.mult, reads=[brp], writes=[brp])
        S.copy("dve", ki[:], kf[:], reads=[brp], writes=[brp])
        S.copy("dve", kf[:], ki[:], reads=[brp], writes=[brp])
        S.stt(rr[:], kf[:], -C1, ang[:], ALU.mult, ALU.add, reads=[brp], writes=[brp])
        S.stt(rr[:], kf[:], -C2, rr[:], ALU.mult, ALU.add, reads=[brp], writes=[brp])
        PI_S = 3.1415925
        S.ts("dve", rr[:], rr[:], PI_S, -PI_S, ALU.min, ALU.max, reads=[brp], writes=[brp])
        S.act(sin2[:, :, 0:16], rr[:], AF.Sin, reads=[brp], writes=[brp])
        S.ts("dve", rc[:], rr[:], math.pi / 2, None, ALU.add, reads=[brp], writes=[brp])
        S.ts("dve", mk[:], rc[:], math.pi, None, ALU.is_gt, reads=[brp], writes=[brp])
        S.stt(rc[:], mk[:], -TWO_PI, rc[:], ALU.mult, ALU.add, reads=[brp], writes=[brp])
        S.ts("dve", rc[:], rc[:], PI_S, -PI_S, ALU.min, ALU.max, reads=[brp], writes=[brp])
        S.act(cos2[:, :, 0:16], rc[:], AF.Sin, reads=[brp], writes=[brp])
        S.copy("dve", sin2[:, :, 16:32], sin2[:, :, 0:16], reads=[brp], writes=[brp])
        S.copy("dve", cos2[:, :, 16:32], cos2[:, :, 0:16], reads=[brp], writes=[brp])

        xs = [S.sb(f"xs{i}", [128, D], F32) for i in range(2)]
        bxs = S.bufs_n(2, "xs")
        junk = S.sb("junk", [128, D], BF16)
        bjunk = S.buf("junk")
        hn = S.sb("hn", [128, D], BF16)
        bhn = S.buf("hn")
        st = S.sb("st", [128, 8], F32)
        bst = S.buf("st")
        hT = [S.sb(f"hT{i}", [128, 8, 512], BF16) for i in range(2)]
        bhT = S.bufs_n(2, "hT")
        pT = S.ps("pT", [128, 8, 128], BF16)
        bpT = S.buf("ps_pT")
        pm = S.ps("pm", [128, 512], F32)
        bpm = S.buf("ps_pm")
        ptr = S.ps("ptr", [128, 8, 128], BF16)
        bptr = S.buf("ps_ptr")
        pq = S.ps("pq", [128, 2, 512], F32)
        bpq = S.buf("ps_pq")
        pf = [S.ps(f"pf{i}", [128, 512], F32) for i in range(2)]
        bpf = S.bufs_n(2, "ps_pf")
        cn = S.sb("cn", [128, 384], BF16)
        bcn = S.buf("cn")
        krn = S.sb("krn", [128, 32], F32)
        krA = S.sb("krA", [128, 32], F32)
        krB = S.sb("krB", [128, 32], F32)
        krf = S.sb("krf", [128, 32], BF16)
        bkr = S.buf("kr")
        cT = S.sb("cT", [128, 3, 128], BF16)
        bcT = S.buf("cT")
        qsb = S.sb("qsb", [128, 8, 96], F32)
        bq = S.buf("q")
        sq = S.sb("sq", [128, 8, 96], F32)
        bsq = S.buf("sq")
        s16 = S.sb("s16", [128, 16], F32)
        bs16 = S.buf("s16")
        qA = S.sb("qA", [128, 8, 32], F32)
        qB = S.sb("qB", [128, 8, 32], F32)
        qf = S.sb("qf", [128, 8, 96], BF16)
        bqf = S.buf("qf")
        ksb = S.sb("ksb", [128, 8, 64], F32)
        bk = S.buf("k")
        kfin = S.sb("kfin", [128, 8, 96], BF16)
        bkf = S.buf("kf")
        vt = [S.sb(f"vt{i}", [128, 8, 65], BF16) for i in range(2)]
        bvt = S.bufs_n(2, "vt")
        for i in range(2):
            S.memset("pool", vt[i][:], 1.0, writes=[bvt[i]])
        qTs = [S.sb(f"qTs{i}", [96, 8, 128], BF16) for i in range(2)]
        bqT = S.bufs_n(2, "qTs")
        kTs = [S.sb(f"kTs{i}", [96, 8, 128], BF16) for i in range(2)]
        bkT = S.bufs_n(2, "kTs")
        rws = [S.sb(f"rws{i}", [128, 512], F32) for i in range(3)]
        brws = S.bufs_n(3, "rws")
        gts = [S.sb(f"gts{i}", [128, 512], BF16) for i in range(3)]
        bgts = S.bufs_n(3, "gts")
        qT_v = qT.rearrange("h d s -> d h s")
        kT_v = kT.rearrange("h d s -> d h s")

        rwch = [(RW0 + j * 128, 128) for j in range(12)]
        rwch += [(RW0 + 1536, 128), (RW0 + 1664, 64), (RW0 + 1728, 128), (RW0 + 1856, 128)]

        for blk in range(NB):
            hb = hT[blk % 2]
            bhb = bhT[blk % 2]
            for tl in range(4):
                t = blk * 4 + tl
                xb = xs[t % 2]
                bx = bxs[t % 2]
                S.dma("sp", xb[:], x[t * 128:(t + 1) * 128, :], bx, writes=[bx])
                S.act(junk[:], xb[:], AF.Square, reads=[bx], writes=[bjunk, bst], accum_out=st[:, 0:1])
                S.act(st[:, 0:1], st[:, 0:1], AF.Sqrt, reads=[bst], writes=[bst], scale=1.0 / D, bias=NORM_EPS)
                S.recip(st[:, 0:1], st[:, 0:1], reads=[bst], writes=[bst])
                S.stt(hn[:], xb[:], st[:, 0:1], g_bc[:], ALU.mult, ALU.mult, reads=[bx, bst, bg], writes=[bhn])
                for kc in range(8):
                    S.tr(pT[:, kc, :], hn[:, kc * 128:(kc + 1) * 128], identb[:], reads=[bhn], writes=[bpT])
                S.copy("act", hb[:, :, tl * 128:(tl + 1) * 128], pT[:], reads=[bpT], writes=[bhb])
                for kc in range(8):
                    S.mm(pm[:, 0:MLA_COLS], hb[:, kc, tl * 128:(tl + 1) * 128], win[:, kc, 0:MLA_COLS],
                         start=(kc == 0), stop=(kc == 7), reads=[bhb, bwin], writes=[bpm])
                S.act(junk[:, 0:256], pm[:, 0:256], AF.Square, reads=[bpm], writes=[bjunk, bst], accum_out=st[:, 1:2])
                S.act(junk[:, 0:128], pm[:, 256:384], AF.Square, reads=[bpm], writes=[bjunk, bst], accum_out=st[:, 2:3])
                S.act(junk[:, 0:32], pm[:, 384:416], AF.Square, reads=[bpm], writes=[bjunk, bst], accum_out=st[:, 3:4])
                S.act(st[:, 1:2], st[:, 1:2], AF.Sqrt, reads=[bst], writes=[bst], scale=1.0 / QLORA, bias=NORM_EPS)
                S.act(st[:, 2:3], st[:, 2:3], AF.Sqrt, reads=[bst], writes=[bst], scale=1.0 / KVLORA, bias=NORM_EPS)
                S.act(st[:, 3:4], st[:, 3:4], AF.Sqrt, reads=[bst], writes=[bst], scale=1.0 / ROPE, bias=NORM_EPS)
                S.recip(st[:, 1:4], st[:, 1:4], reads=[bst], writes=[bst])
                S.stt(cn[:, 0:256], pm[:, 0:256], st[:, 1:2], gq_bc[:], ALU.mult, ALU.mult, reads=[bpm, bst], writes=[bcn])
                S.stt(cn[:, 256:384], pm[:, 256:384], st[:, 2:3], gkv_bc[:], ALU.mult, ALU.mult, reads=[bpm, bst], writes=[bcn])
                S.stt(krn[:], pm[:, 384:416], st[:, 3:4], gkr_bc[:], ALU.mult, ALU.mult, reads=[bpm, bst], writes=[bkr])
                S.tt("dve", krA[:], krn[:], cos2[:, t, :], ALU.mult, reads=[bkr, brp], writes=[bkr])
                S.tt("dve", krB[:], krn[:], sin2[:, t, :], ALU.mult, reads=[bkr, brp], writes=[bkr])
                S.tt("dve", krf[:, 0:16], krA[:, 0:16], krB[:, 16:32], ALU.subtract, reads=[bkr], writes=[bkr])
                S.tt("dve", krf[:, 16:32], krA[:, 16:32], krB[:, 0:16], ALU.add, reads=[bkr], writes=[bkr])
                for j in range(3):
                    S.tr(ptr[:, j, :], cn[:, j * 128:(j + 1) * 128], identb[:], reads=[bcn], writes=[bptr])
                S.copy("act", cT[:], ptr[:, 0:3, :], reads=[bptr], writes=[bcT])
                for hf in range(2):
                    for kc in range(2):
                        S.mm(pq[:, hf, 0:384], cT[:, kc, :], wuq[:, kc, hf * 384:(hf + 1) * 384],
                             start=(kc == 0), stop=(kc == 1), reads=[bcT, bsm], writes=[bpq])
                for hf in range(2):
                    S.copy("act", qsb[:, hf * 4:(hf + 1) * 4, :],
                           pq[:, hf, 0:384].rearrange("p (h d) -> p h d", d=96), reads=[bpq], writes=[bq])
                S.tt("dve", sq[:], qsb[:], qsb[:], ALU.mult, reads=[bq], writes=[bsq])
                S.red(s16[:, 0:8], sq[:, :, 0:64], reads=[bsq], writes=[bs16])
                S.red(s16[:, 8:16], sq[:, :, 64:96], reads=[bsq], writes=[bs16])
                S.act(s16[:, 0:8], s16[:, 0:8], AF.Sqrt, reads=[bs16], writes=[bs16], scale=1.0 / NOPE, bias=NORM_EPS)
                S.act(s16[:, 8:16], s16[:, 8:16], AF.Sqrt, reads=[bs16], writes=[bs16], scale=1.0 / ROPE, bias=NORM_EPS)
                S.recip(s16[:], s16[:], reads=[bs16], writes=[bs16])
                S.tt("dve", qsb[:, :, 0:64], qsb[:, :, 0:64], bc(s16[:, 0:8], 2, [128, 8, 64]), ALU.mult, reads=[bq, bs16], writes=[bq])
                S.tt("dve", qsb[:, :, 64:96], qsb[:, :, 64:96], bc(s16[:, 8:16], 2, [128, 8, 32]), ALU.mult, reads=[bq, bs16], writes=[bq])
                S.tt("dve", qsb[:], qsb[:], bc(gqh_bc[:], 1, [128, 8, 96]), ALU.mult, reads=[bq, bg], writes=[bq])
                S.copy("act", qf[:, :, 0:64], qsb[:, :, 0:64], reads=[bq], writes=[bqf])
                S.tt("dve", qA[:], qsb[:, :, 64:96], bc(cos2[:, t, :], 1, [128, 8, 32]), ALU.mult, reads=[bq, brp], writes=[bsq])
                S.tt("dve", qB[:], qsb[:, :, 64:96], bc(sin2[:, t, :], 1, [128, 8, 32]), ALU.mult, reads=[bq, brp], writes=[bsq])
                S.tt("dve", qf[:, :, 64:80], qA[:, :, 0:16], qB[:, :, 16:32], ALU.subtract, reads=[bsq], writes=[bqf])
                S.tt("dve", qf[:, :, 80:96], qA[:, :, 16:32], qB[:, :, 0:16], ALU.add, reads=[bsq], writes=[bqf])
                for hf in range(2):
                    S.mm(pq[:, hf, :], cT[:, 2, :], wukv[:, hf, :, :].rearrange("k h d -> k (h d)"),
                         reads=[bcT, bsm2, bq], writes=[bpq])
                vb = vt[t % 2]
                bv = bvt[t % 2]
                S.copy("act", ksb[:], pq[:, 0, :].rearrange("p (h d) -> p h d", d=64), reads=[bpq], writes=[bk])
                S.copy("act", vb[:, :, 0:64], pq[:, 1, :].rearrange("p (h d) -> p h d", d=64), reads=[bpq], writes=[bv])
                S.dma("sp", vaug[t * 128:(t + 1) * 128, :], vb[:].rearrange("p h d -> p (h d)"), bv, reads=[bv])
                S.tt("dve", sq[:, :, 0:64], ksb[:], ksb[:], ALU.mult, reads=[bk], writes=[bsq])
                S.red(s16[:, 0:8], sq[:, :, 0:64], reads=[bsq], writes=[bs16])
                S.act(s16[:, 0:8], s16[:, 0:8], AF.Sqrt, reads=[bs16], writes=[bs16], scale=1.0 / NOPE, bias=NORM_EPS)
                S.recip(s16[:, 0:8], s16[:, 0:8], reads=[bs16], writes=[bs16])
                S.tt("dve", ksb[:], ksb[:], bc(s16[:, 0:8], 2, [128, 8, 64]), ALU.mult, reads=[bk, bs16], writes=[bk])
                S.tt("dve", kfin[:, :, 0:64], ksb[:], bc(gkn_bc[:], 1, [128, 8, 64]), ALU.mult, reads=[bk, bg], writes=[bkf])
                S.copy("dve", kfin[:, :, 64:96], bc(krf[:], 1, [128, 8, 32]), reads=[bkr], writes=[bkf])
                qo = qTs[t % 2]
                bqo = bqT[t % 2]
                ko = kTs[t % 2]
                bko = bkT[t % 2]
                for h in range(8):
                    S.tr(ptr[0:96, h, :], qf[:, h, :], identb[:], reads=[bqf], writes=[bptr])
                S.copy("act", qo[:], ptr[0:96, :, :], reads=[bptr], writes=[bqo])
                S.dma("sp", qT_v[:, :, t * 128:(t + 1) * 128], qo[:], bqo, reads=[bqo])
                for h in range(8):
                    S.tr(ptr[0:96, h, :], kfin[:, h, :], identb[:], reads=[bkf], writes=[bptr])
                S.copy("act", ko[:], ptr[0:96, :, :], reads=[bptr], writes=[bko])
                S.dma("sp", kT_v[:, :, t * 128:(t + 1) * 128], ko[:], bko, reads=[bko])
            n = 0
            for j, (c0, wd) in enumerate(rwch):
                p = pf[n % 2]
                bp = bpf[n % 2]
                for kc in range(8):
                    S.mm(p[0:wd, :], win[:, kc, c0:c0 + wd], hb[:, kc, :], start=(kc == 0), stop=(kc == 7),
                         reads=[bwin, bhb], writes=[bp])
                o = rws[n % 3]
                bo = brws[n % 3]
                eng = "dve" if n % 2 == 0 else "act"
                S.copy(eng, o[0:wd, :], p[0:wd, :], reads=[bp], writes=[bo])
                S.dma("sp", prw[j, 0:wd, blk * 512:(blk + 1) * 512], o[0:wd, :], bo, reads=[bo])
                n += 1
            for j in range(16):
                c0 = G0 + j * 128
                p = pf[n % 2]
                bp = bpf[n % 2]
                for kc in range(8):
                    S.mm(p[:, :], win[:, kc, c0:c0 + 128], hb[:, kc, :], start=(kc == 0), stop=(kc == 7),
                         reads=[bwin, bhb], writes=[bp])
                o = gts[n % 3]
                bo = bgts[n % 3]
                S.act(o[:], p[:], AF.Sigmoid, reads=[bp, bg], writes=[bo], bias=bgate[:, j:j + 1])
                S.dma("sp", gat[j, :, blk * 512:(blk + 1) * 512], o[:], bo, reads=[bo])
                n += 1
        S.end()

    phase_a()
    if stop_after == "A":
        gst.close()
        return nc

    def phase_att():
        S.begin()
        va = S.sb("va", [128, NT, HEADS * 65], BF16)
        bva = S.buf("va")
        va_v = vaug.rearrange("(t p) c -> p t c", p=128)
        for t4 in range(0, NT, 4):
            S.dma("sp", va[:, t4:t4 + 4, :], va_v[:, t4:t4 + 4, :], bva, writes=[bva])
        kh = [S.sb(f"kh{i}", [96, S_LEN], BF16) for i in range(2)]
        bkh = S.bufs_n(2, "kh")
        qh = [S.sb(f"qh{i}", [96, S_LEN], BF16) for i in range(2)]
        bqh = S.bufs_n(2, "qh")
        NPS = 5
        pS = [S.ps(f"pS{i}", [128, 512], F32) for i in range(NPS)]
        bpS = S.bufs_n(NPS, "ps_S")
        NPT = 6
        pt = [S.sb(f"pt{i}", [128, 512], BF16) for i in range(NPT)]
        bpt = S.bufs_n(NPT, "pt")
        pacc = [S.ps(f"pacc{i}", [128, 512], F32) for i in range(2)]
        bacc = S.bufs_n(2, "ps_acc")
        pbc = S.ps("pbc", [64, 512], F32)
        bpbc = S.buf("ps_bc")
        rden = S.sb("rden", [128, 512], F32)
        brden = S.buf("rden")
        bcs = S.sb("bcs", [64, 512], F32)
        bbcs = S.buf("bcs")
        oas = [S.sb(f"oas{i}", [64, 512], BF16) for i in range(2)]
        boas = S.bufs_n(2, "oas")
        scale = 1.0 / math.sqrt(NOPE + ROPE)
        it = 0
        for h in range(HEADS):
            k_ = kh[h % 2]
            bk_ = bkh[h % 2]
            q_ = qh[h % 2]
            bq_ = bqh[h % 2]
            S.dma("sp", k_[:], kT[h, :, :], bk_, writes=[bk_])
            S.dma("sp", q_[:], qT[h, :, :], bq_, writes=[bq_])
            for qb in range(NB):
                acc = pacc[it % 2]
                ba_ = bacc[it % 2]
                it += 1
                qsl = q_[:, qb * 512:(qb + 1) * 512]

                def qk(kt):
                    ps_ = pS[kt % NPS]
                    S.mm(ps_[:], k_[:, kt * 128:(kt + 1) * 128], qsl, reads=[bk_, bq_], writes=[bpS[kt % NPS]])

                def pv(kt):
                    ps_ = pS[kt % NPS]
                    p_ = pt[kt % NPT]
                    S.act(p_[:], ps_[:], AF.Exp, reads=[bpS[kt % NPS]], writes=[bpt[kt % NPT]], scale=scale)
                    S.mm(acc[0:65, :], va[:, kt, h * 65:(h + 1) * 65], p_[:], start=(kt == 0), stop=(kt == NT - 1),
                         reads=[bva, bpt[kt % NPT]], writes=[ba_])

                LA = 4
                for kt in range(min(LA, NT)):
                    qk(kt)
                for kt in range(NT):
                    pv(kt)
                    if kt + LA < NT:
                        qk(kt + LA)
                S.recip(rden[64:65, :], acc[64:65, :], reads=[ba_], writes=[brden])
                S.mm(pbc[:], ones[64:65, 0:64], rden[64:65, :], reads=[brden], writes=[bpbc])
                S.copy("act", bcs[:], pbc[:], reads=[bpbc], writes=[bbcs])
                o_ = oas[it % 2]
                bo_ = boas[it % 2]
                S.tt("dve", o_[:], acc[0:64, :], bcs[:], ALU.mult, reads=[ba_, bbcs], writes=[bo_])
                S.dma("sp", oaT[h, :, qb * 512:(qb + 1) * 512], o_[:], bo_, reads=[bo_])
        S.end()

    phase_att()
    if stop_after == "ATT":
        gst.close()
        return nc


    rwrel = [(j * 128, 128) for j in range(12)] + [(1536, 128), (1664, 64), (1728, 128), (1856, 128)]
    RDT = BF16 if rbf16 else F32
    IDT = BF16 if ibf16 else F32
    ard = dscr("ard", [2, 128, 4, NCK, 2, CH], RDT)
    btd = dscr("btd", [2, 128, 4, S_LEN], RDT)
    ktd = dscr("ktd", [2, 128, 4, S_LEN], RDT)
    vrd = dscr("vrd", [128, 4, S_LEN], RDT)
    gamd = dscr("gamd", [2, 128, 4, NCK], F32)
    bond = dscr("bond", [128, 4, S_LEN], F32)
    gtd = dscr("gtd", [2, 128, 4, S_LEN], BF16)
    orawd = dscr("orawd", [2, 128, 4, S_LEN], F32)
    ofT_v = ofT.rearrange("a p s -> p a s")
    obT_v = obT.rearrange("a p s -> p a s")

    def rw_params(extra_vecs):
        bpar = S.buf("par")
        par = S.sb("par", [128, 4 * len(extra_vecs)], F32)
        for ci, vec in enumerate(extra_vecs):
            if vec is None:
                continue
            for p in range(4):
                load_col("sp", par[:, ci * 4 + p:ci * 4 + p + 1], vec[p * 128:(p + 1) * 128], bpar)
        return par, bpar

    def make_blk2(bcst):
        blk2 = S.sb("blk2", [128, 128], F32)
        S.memset("pool", blk2[:], 0.0, writes=[bcst])
        S.memset("pool", blk2[0:64, 0:64], 1.0, writes=[bcst])
        S.memset("pool", blk2[64:128, 64:128], 1.0, writes=[bcst])
        return blk2

    def phase_rprep():
        RB = 256
        NCB = RB // CH
        S.begin()
        par, bpar = rw_params([a0, k_k, k_a, None, w0[0], r_k, w0[1]])
        S.ts("dve", par[:, 12:16], par[:, 8:12], -1.0, 1.0, ALU.mult, ALU.add, reads=[bpar], writes=[bpar])
        mu = S.sb("mu", [128, 16, 3], F32)
        S.memset("pool", mu[:], 0.0, writes=[bpar])
        for j, (c0, wd) in enumerate(rwrel):
            load_col("sp", mu[0:wd, j, 1:2], shift_mu[0, c0:c0 + wd], bpar, n=wd)
            load_col("sp", mu[0:wd, j, 2:3], shift_mu[1, c0:c0 + wd], bpar, n=wd)
        S.ts("dve", mu[:, :, 0], mu[:, :, 1], -1.0, 1.0, ALU.mult, ALU.add, reads=[bpar], writes=[bpar])
        S.tt("dve", mu[:, :, 0], mu[:, :, 0], mu[:, :, 2], ALU.subtract, reads=[bpar], writes=[bpar])
        bw = S.buf("w")
        a2_sb = S.sb("a2_sb", [64, 512], F32)
        w2_sb = S.sb("w2_sb", [128, 512], F32)
        g2_sb = [S.sb(f"g2_sb{d}", [128, 512], F32) for d in range(2)]
        S.dma("sp", a2_sb[:], a2[:, :], bw, writes=[bw])
        S.dma("sp", w2_sb[64:128, :], w2[0, :, :], bw, writes=[bw])
        S.dma("sp", w2_sb[0:64, :], w2[1, :, :], bw, writes=[bw])
        for d in range(2):
            S.dma("sp", g2_sb[d][:], g2[d, :, :], bw, writes=[bw])
        bcst = S.buf("const")
        blk2 = make_blk2(bcst)
        blk_rk = S.sb("blk_rk", [128, 4, 128], F32)
        for p in range(4):
            S.ts("dve", blk_rk[:, p, :], blk2[:], par[:, 20 + p:21 + p], None, ALU.mult, reads=[bcst, bpar], writes=[bcst])
        rmask = S.sb("rmask", [128, 4 * RB], F32)
        S.memset("pool", rmask[:], 1.0, writes=[bcst])
        S.op("pool", lambda e: e.affine_select(rmask[:].rearrange("p (a b) -> p a b", b=CH), rmask[:].rearrange("p (a b) -> p a b", b=CH),
                                               [[0, 4 * RB // CH], [1, CH]], ALU.is_gt, 0.0, base=0, channel_multiplier=0),
             reads=[bcst], writes=[bcst])
        prw_v = prw.rearrange("j p s -> p j s")
        sh4 = [128, 4, RB]

        def pcol(c):
            return par[:, c:c + 4].unsqueeze(2).to_broadcast(sh4)

        def v4(ap):
            return ap.rearrange("p a (c t) -> p a c t", t=CH)

        def blk_thread(tid):
            T = f"t{tid}"
            pb = S.sb(T + "pb", [128, 16, RB + 2], F32)
            bpb = S.buf(T + "pb")
            S.memset("pool", pb[:], 0.0, writes=[bpb])
            u = S.sb(T + "u", [128, 16, RB], F32)
            bu, buk, buv = S.buf(T + "u"), S.buf(T + "uk"), S.buf(T + "uv")
            A = [S.sb(T + f"A{i}", [128, 4, RB], F32) for i in range(8)]
            bA = S.bufs_n(8, T + "A")
            vr = S.sb(T + "vr", [128, 4, RB], RDT)
            bvr = S.buf(T + "vr")
            btr = [S.sb(T + f"btr{d}", [128, 4, RB], RDT) for d in range(2)]
            ktr = [S.sb(T + f"ktr{d}", [128, 4, RB], RDT) for d in range(2)]
            gts = [S.sb(T + f"gts{d}", [128, 4, RB], BF16) for d in range(2)]
            ar = [S.sb(T + f"ar{d}", [128, 4, NCB, 2, CH], RDT) for d in range(2)]
            gam = [S.sb(T + f"gam{d}", [128, 4, NCB], F32) for d in range(2)]
            bbtr, bktr, bgts, bar, bgam = [S.bufs_n(2, T + nm) for nm in ("btr", "ktr", "gts", "ar", "gam")]
            tot = S.sb(T + "tot", [128, 4, NCB], F32)
            btot = S.buf(T + "tot")
            pp = [S.ps(T + f"pp{i}", [128, 512], F32) for i in range(4)]
            bpp = S.bufs_n(4, "ps_pp" + T)
            npp = [0]

            def nextpp():
                i = npp[0] % 4
                npp[0] += 1
                return pp[i], bpp[i]

            for blk in range(tid, S_LEN // RB, 2):
                t0 = blk * RB
                c0 = t0 // CH
                lo = max(t0 - 1, 0)
                hi = min(t0 + RB + 1, S_LEN)
                dlo = lo - (t0 - 1)
                if t0 == 0:
                    S.memset("pool", pb[:, :, 0:1], 0.0, writes=[bpb])
                if t0 + RB == S_LEN:
                    S.memset("pool", pb[:, :, RB + 1:RB + 2], 0.0, writes=[bpb])
                S.dma("sp", pb[:, 0:13, dlo:dlo + (hi - lo)], prw_v[:, 0:13, lo:hi], bpb, writes=[bpb])
                S.dma("sp", pb[0:64, 13, dlo:dlo + (hi - lo)], prw_v[0:64, 13, lo:hi], bpb, writes=[bpb])
                S.dma("sp", pb[:, 14:16, dlo:dlo + (hi - lo)], prw_v[:, 14:16, lo:hi], bpb, writes=[bpb])
                for j in range(16):
                    wd = rwrel[j][1]
                    S.act(u[0:wd, j, :], pb[0:wd, j, 1:RB + 1], AF.Identity, reads=[bpb, bpar], writes=[bu, buk, buv], scale=mu[0:wd, j, 0:1])
                for j in range(16):
                    wd = rwrel[j][1]
                    S.stt(u[0:wd, j, :], pb[0:wd, j, 0:RB], mu[0:wd, j, 1:2], u[0:wd, j, :], ALU.mult, ALU.add, reads=[bpb, bpar, bu], writes=[bu, buk, buv])
                    S.stt(u[0:wd, j, :], pb[0:wd, j, 2:RB + 2], mu[0:wd, j, 2:3], u[0:wd, j, :], ALU.mult, ALU.add, reads=[bpb, bpar, bu], writes=[bu, buk, buv])
                yield
                r_ = u[:, 0:4, :]
                k_ = u[:, 4:8, :]
                v_ = u[:, 8:12, :]
                S.copy("act", vr[:], v_, reads=[buv], writes=[bvr])
                S.dma("sp", vrd[:, :, t0:t0 + RB], vr[:], bvr, reads=[bvr])
                for p in range(4):
                    pq_, bq_ = nextpp()
                    S.mm(pq_[:, 0:RB], a2_sb[0:64, p * 128:(p + 1) * 128], u[0:64, 12, :], reads=[bw, bu], writes=[bq_])
                    S.act(A[0][:, p, :], pq_[:, 0:RB], AF.Sigmoid, reads=[bq_, bpar], writes=[bA[0]], bias=par[:, p:p + 1])
                yield
                S.tt("dve", A[1][:], k_, pcol(4), ALU.mult, reads=[buk, bpar], writes=[bA[1]])
                S.act(A[6][:], A[1][:], AF.Square, reads=[bA[1]], writes=[bA[6]])
                for p in range(4):
                    pq_, bq_ = nextpp()
                    S.mm(pq_[:, 0:RB], blk2[:], A[6][:, p, :], reads=[bcst, bA[6]], writes=[bq_])
                    S.act(A[4][:, p, :], pq_[:, 0:RB], AF.Sqrt, reads=[bq_], writes=[bA[4]])
                S.ts("dve", A[4][:], A[4][:], 1e-12, None, ALU.max, reads=[bA[4]], writes=[bA[4]])
                S.recip(A[4][:], A[4][:], reads=[bA[4]], writes=[bA[4]])
                S.tt("dve", A[1][:], A[1][:], A[4][:], ALU.mult, reads=[bA[1], bA[4]], writes=[bA[1]])
                yield
                S.tt("dve", A[6][:], A[0][:], pcol(8), ALU.mult, reads=[bA[0], bpar], writes=[bA[6]])
                S.tt("dve", A[6][:], A[6][:], pcol(12), ALU.add, reads=[bA[6], bpar], writes=[bA[6]])
                S.tt("dve", k_, k_, A[6][:], ALU.mult, reads=[buk, bA[6]], writes=[buk])
                S.tt("dve", A[3][:], A[1][:], A[0][:], ALU.mult, reads=[bA[1], bA[0]], writes=[bA[3]])
                S.act(A[1][:], A[1][:], AF.Identity, reads=[bA[1]], writes=[bA[1]], scale=-1.0)
                yield
                S.tt("dve", A[6][:], r_, k_, ALU.mult, reads=[bu, buk], writes=[bA[6]])
                for p in range(4):
                    pq_, bq_ = nextpp()
                    S.mm(pq_[:, 0:RB], blk_rk[:, p, :], A[6][:, p, :], reads=[bcst, bA[6]], writes=[bq_])
                    S.tt("dve", A[0][:, p, :], pq_[:, 0:RB], u[:, 8 + p, :], ALU.mult, reads=[bq_, buv], writes=[bA[0]])
                S.dma("sp", bond[:, :, t0:t0 + RB], A[0][:], bA[0], reads=[bA[0]])
                yield
                for d in range(2):
                    dp0 = 64 if d == 0 else 0
                    dl_chunk = 12 if d == 0 else 13
                    gl_chunk = 14 if d == 0 else 15
                    w0c = 16 if d == 0 else 24
                    S.act(A[6][:, 0, :], u[:, gl_chunk, :], AF.Sigmoid, reads=[bu], writes=[bA[6]])
                    for p in range(4):
                        pq_, bq_ = nextpp()
                        S.mm(pq_[:, 0:RB], g2_sb[d][:, p * 128:(p + 1) * 128], A[6][:, 0, :], reads=[bw, bA[6]], writes=[bq_])
                        S.copy("act", gts[d][:, p, :], pq_[:, 0:RB], reads=[bq_], writes=[bgts[d]])
                    S.dma("sp", gtd[d, :, :, t0:t0 + RB], gts[d][:], bgts[d], reads=[bgts[d]])
                    yield
                    S.act(A[6][dp0:dp0 + 64, 1, :], u[dp0:dp0 + 64, dl_chunk, :], AF.Tanh, reads=[bu], writes=[bA[6]])
                    for p in range(4):
                        pq_, bq_ = nextpp()
                        S.mm(pq_[:, 0:RB], w2_sb[dp0:dp0 + 64, p * 128:(p + 1) * 128], A[6][dp0:dp0 + 64, 1, :], reads=[bw, bA[6]], writes=[bq_])
                        S.act(A[4][:, p, :], pq_[:, 0:RB], AF.Sigmoid, reads=[bq_, bpar], writes=[bA[4]], bias=par[:, w0c + p:w0c + p + 1])
                    S.ts("dve", A[4][:], A[4][:], -math.exp(-0.5), None, ALU.mult, reads=[bA[4]], writes=[bA[4]])
                    yield
                    fl = lambda t_: t_[:].rearrange("p a t -> p (a t)")
                    S.op("dve", (lambda o_, m_, i_: lambda e: e.tensor_tensor_scan(o_, m_, i_, 0.0, ALU.mult, ALU.add))(fl(A[5]), rmask[:], fl(A[4])),
                         reads=[bA[4], bcst], writes=[bA[5]])
                    if d == 1:
                        S.copy("dve", tot[:], v4(A[5][:])[:, :, :, CH - 1], reads=[bA[5]], writes=[btot])
                        S.tt("dve", A[5][:], A[4][:], A[5][:], ALU.subtract, reads=[bA[4], bA[5]], writes=[bA[5]])
                        S.tt("dve", v4(A[5][:]), v4(A[5][:]), tot[:].unsqueeze(3).to_broadcast([128, 4, NCB, CH]), ALU.add,
                             reads=[bA[5], btot], writes=[bA[5]])
                    yield
                    S.tt("dve", A[6][:], A[5][:], A[4][:], ALU.subtract, reads=[bA[5], bA[4]], writes=[bA[6]])
                    S.act(A[6][:], A[6][:], AF.Exp, reads=[bA[6]], writes=[bA[6]])
                    S.tt("dve", ar[d][:, :, :, 0, :], v4(A[1][:]), v4(A[6][:]), ALU.mult, reads=[bA[1], bA[6]], writes=[bar[d]])
                    S.act(A[7][:], A[5][:], AF.Exp, reads=[bA[5]], writes=[bA[7]])
                    S.tt("pool", ar[d][:, :, :, 1, :], v4(r_), v4(A[7][:]), ALU.mult, reads=[bu, bA[7]], writes=[bar[d]])
                    gidx = CH - 1 if d == 0 else 0
                    S.copy("dve", gam[d][:], v4(A[7][:])[:, :, :, gidx], reads=[bA[7]], writes=[bgam[d]])
                    S.act(A[6][:], A[5][:], AF.Exp, reads=[bA[5]], writes=[bA[6]], scale=-1.0)
                    S.tt("dve", btr[d][:], A[3][:], A[6][:], ALU.mult, reads=[bA[3], bA[6]], writes=[bbtr[d]])
                    S.tt("pool", ktr[d][:], k_, A[6][:], ALU.mult, reads=[buk, bA[6]], writes=[bktr[d]])
                    yield
                    for p in range(4):
                        S.dma("sp", ard[d, :, p, c0:c0 + NCB, :, :], ar[d][:, p, :, :, :], bar[d], reads=[bar[d]])
                    S.dma("sp", btd[d, :, :, t0:t0 + RB], btr[d][:], bbtr[d], reads=[bbtr[d]])
                    S.dma("sp", ktd[d, :, :, t0:t0 + RB], ktr[d][:], bktr[d], reads=[bktr[d]])
                    S.dma("sp", gamd[d, :, :, c0:c0 + NCB], gam[d][:], bgam[d], reads=[bgam[d]])
            yield

        threads = [blk_thread(0), blk_thread(1)]
        while threads:
            for g in list(threads):
                try:
                    next(g)
                except StopIteration:
                    threads.remove(g)
        S.end()

    def phase_rscan(dirs=(0, 1)):
        RB = 256
        NRB = S_LEN // RB
        NCB = RB // CH
        S.begin()
        bcst = S.buf("const")
        I2 = S.sb("I2", [128, 64], F32)
        for hf in range(2):
            r0 = hf * 64
            S.copy("dve", I2[r0:r0 + 64, :], ident[r0:r0 + 64, r0:r0 + 64], writes=[bcst])
        identr = S.sb("identr", [128, 128], RDT)
        S.copy("dve", identr[:], ident[:], writes=[bcst])

        def dir_thread(d):
            T = f"d{d}"
            bmk = S.buf(T + "mask")
            mask2 = S.sb(T + "mask2", [128, 128], F32)
            maskY = S.sb(T + "maskY", [128, 64], F32)
            for hf in range(2):
                r0 = hf * 64
                on = ones[r0:r0 + 64, 0:64]
                if d == 0:
                    pat, cm = [[1, 64]], -1
                else:
                    pat, cm = [[-1, 64]], 1
                S.op("pool", (lambda o_, i_, pat_, cm_: lambda e: e.affine_select(o_, i_, pat_, ALU.is_gt, 0.0, base=0, channel_multiplier=cm_))(mask2[r0:r0 + 64, 0:64], on, pat, cm), writes=[bmk])
                S.op("pool", (lambda o_, i_, pat_, cm_: lambda e: e.affine_select(o_, i_, pat_, ALU.is_ge, 0.0, base=0, channel_multiplier=cm_))(mask2[r0:r0 + 64, 64:128], on, pat, cm), writes=[bmk])
                pat2 = [[-pat[0][0], 64]]
                S.op("pool", (lambda o_, i_, pat_, cm_: lambda e: e.affine_select(o_, i_, pat_, ALU.is_gt, 0.0, base=0, channel_multiplier=cm_))(maskY[r0:r0 + 64, :], on, pat2, -cm), writes=[bmk])
            Hst = S.sb(T + "Hst", [128, 4, 64], F32)
            Hr = S.sb(T + "Hr", [128, 4, 64], RDT) if RDT != F32 else Hst
            bH = S.buf(T + "H")
            S.memset("pool", Hst[:], 0.0, writes=[bH])
            if RDT != F32:
                S.memset("pool", Hr[:], 0.0, writes=[bH])
            ar_s = [S.sb(f"{T}ar{i}", [128, 4, NCB, 2, CH], RDT) for i in range(2)]
            btr_s = [S.sb(f"{T}btr{i}", [128, 4, RB], RDT) for i in range(2)]
            ktr_s = [S.sb(f"{T}ktr{i}", [128, 4, RB], RDT) for i in range(2)]
            vr_s = [S.sb(f"{T}vr{i}", [128, 4, RB], RDT) for i in range(2)]
            gam_s = [S.sb(f"{T}gam{i}", [128, 4, NCB], F32) for i in range(2)]
            bar_s, bbtr_s, bktr_s, bvr_s, bgam_s = [S.bufs_n(2, T + nm) for nm in ("ar", "btr", "ktr", "vr", "gam")]
            tms = [S.sb(f"{T}tm{i}", [128, NCB, 512], RDT) for i in range(3)]
            btm = S.bufs_n(3, T + "tm")
            A1m = [S.sb(f"{T}A1m{i}", [128, 4, 128], RDT) for i in range(2)]
            A2m = [S.sb(f"{T}A2m{i}", [128, 4, 128], RDT) for i in range(2)]
            bA1m = S.bufs_n(2, T + "A1m")
            bA2m = S.bufs_n(2, T + "A2m")
            Xk = [S.sb(f"{T}Xk{i}", [128, 4, 64], IDT) for i in range(2)]
            Yk = [S.sb(f"{T}Yk{i}", [128, 4, 64], IDT) for i in range(2)]
            IY = [S.sb(f"{T}IY{i}", [128, 4, 64], IDT) for i in range(2)]
            Qk = [S.sb(f"{T}Qk{i}", [128, 4, 64], IDT) for i in range(2)]
            TT = [S.sb(f"{T}TT{i}", [128, 4, 64], RDT) for i in range(2)]
            bXk, bYk, bIY, bQk, bTT = [S.bufs_n(2, T + nm) for nm in ("Xk", "Yk", "IY", "Qk", "TT")]
            Xs = S.sb(T + "Xs", [128, 4, 64], RDT)
            Us = S.sb(T + "Us", [128, 4, 64], RDT)
            bXs, bUs = S.buf(T + "Xs"), S.buf(T + "Us")
            oT = [S.sb(f"{T}oT{i}", [128, 4, RB], F32) for i in range(2)]
            boT = S.bufs_n(2, T + "oT")

            def bank(name):
                return S.ps(T + name, [128, 512], F32)
            bkA, bkY, bkQ, bkS = bank("bA"), bank("bY"), bank("bQ"), bank("bS")
            bpA, bpY, bpQ, bpS = [S.buf("ps_" + T + nm) for nm in "AYQS"]
            pA = bkA[:].rearrange("p (a b) -> p a b", a=4)
            pYX = bkY[:].rearrange("p (t a b) -> p t a b", t=2, a=4)
            ptm = bkQ[:].rearrange("p (a b) -> p a b", a=4)
            pQ = bkQ[:, 0:256].rearrange("p (a b) -> p a b", a=4)
            pS = bkS[:].rearrange("p (t a b) -> p t a b", t=2, a=4)
            bptm = bpQ

            order = list(range(NRB)) if d == 0 else list(range(NRB - 1, -1, -1))
            corder = list(range(NCB)) if d == 0 else list(range(NCB - 1, -1, -1))

            def issue_loads(i):
                blk = order[i]
                sl = i % 2
                t0 = blk * RB
                c0 = t0 // CH
                for p in range(4):
                    S.dma("sp", ar_s[sl][:, p, :, :, :], ard[d, :, p, c0:c0 + NCB, :, :], bar_s[sl], writes=[bar_s[sl]])
                S.dma("sp", btr_s[sl][:], btd[d, :, :, t0:t0 + RB], bbtr_s[sl], writes=[bbtr_s[sl]])
                S.dma("sp", ktr_s[sl][:], ktd[d, :, :, t0:t0 + RB], bktr_s[sl], writes=[bktr_s[sl]])
                S.dma("sp", vr_s[sl][:], vrd[:, :, t0:t0 + RB], bvr_s[sl], writes=[bvr_s[sl]])
                S.dma("sp", gam_s[sl][:], gamd[d, :, :, c0:c0 + NCB], bgam_s[sl], writes=[bgam_s[sl]])

            issue_loads(0)
            yield

            def hloop():
                for h in HORD:
                    p, q0 = h // 2, (h % 2) * 64
                    yield h, p, slice(q0, q0 + 64)

            for i, blk in enumerate(order):
                bs = i % 2
                t0 = blk * RB
                ar, btr, ktr, vr, gam = ar_s[bs], btr_s[bs], ktr_s[bs], vr_s[bs], gam_s[bs]
                bar, bbtr, bktr, bvr, bgam = bar_s[bs], bbtr_s[bs], bktr_s[bs], bvr_s[bs], bgam_s[bs]
                oT_, boT_ = oT[bs], boT[bs]

                def prep_chunk(c):
                    sl = c % 2
                    cs = slice(c * CH, (c + 1) * CH)
                    for h, p, qs in hloop():
                        rhs = ar[qs, p, c, :, :].rearrange("k a t -> k (a t)")
                        S.mm(pA[qs, p, :], btr[qs, p, cs], rhs, reads=[bbtr, bar], writes=[bpA])
                    for h, p, qs in hloop():
                        S.mm(pYX[qs, 0, p, :], ar[qs, p, c, 0, :], btr[qs, p, cs], reads=[bbtr, bar], writes=[bpY])
                    yield
                    S.tt("dve", A1m[sl][:], pA, bc(mask2[:], 1, [128, 4, 128]), ALU.mult, reads=[bpA, bmk], writes=[bA1m[sl]])
                    S.tt("dve", Yk[sl][:], pYX[:, 0, :, :], bc(maskY[:], 1, [128, 4, 64]), ALU.mult, reads=[bpY, bmk], writes=[bYk[sl]])
                    for h, p, qs in hloop():
                        rhs = ar[qs, p, c, :, :].rearrange("k a t -> k (a t)")
                        S.mm(pA[qs, p, :], ktr[qs, p, cs], rhs, reads=[bktr, bar], writes=[bpA])
                    yield
                    S.tt("dve", A2m[sl][:], pA, bc(mask2[:], 1, [128, 4, 128]), ALU.mult, reads=[bpA, bmk], writes=[bA2m[sl]])
                    S.copy("act", Xk[sl][:], A1m[sl][:, :, 0:64], reads=[bA1m[sl]], writes=[bXk[sl]])
                    S.tt("pool", Qk[sl][:], A1m[sl][:, :, 0:64], bc(I2[:], 1, [128, 4, 64]), ALU.add, reads=[bA1m[sl], bcst], writes=[bQk[sl]])
                    yield
                    for lvl in range(1, 6):
                        last = lvl == 5
                        for h, p, qs in hloop():
                            if not last:
                                S.mm(pYX[qs, 1, p, :], Yk[sl][qs, p, :], Xk[sl][qs, p, :], reads=[bYk[sl], bXk[sl]], writes=[bpY])
                            S.mm(pYX[qs, 0, p, :], Xk[sl][qs, p, :], Yk[sl][qs, p, :], reads=[bYk[sl], bXk[sl]], writes=[bpY])
                        yield
                        S.tt("dve", IY[sl][:], pYX[:, 0, :, :], bc(I2[:], 1, [128, 4, 64]), ALU.add, reads=[bpY, bcst], writes=[bIY[sl]])
                        if not last:
                            S.copy("act", Xk[sl][:], pYX[:, 1, :, :], reads=[bpY], writes=[bXk[sl]])
                            S.copy("act", Yk[sl][:], pYX[:, 0, :, :], reads=[bpY], writes=[bYk[sl]])
                        yield
                        for h, p, qs in hloop():
                            S.mm(pQ[qs, p, :], IY[sl][qs, p, :], Qk[sl][qs, p, :], reads=[bIY[sl], bQk[sl]], writes=[bpQ])
                        yield
                        if last:
                            S.copy("act", TT[sl][:], pQ, reads=[bpQ], writes=[bTT[sl]])
                        else:
                            S.copy("act", Qk[sl][:], pQ, reads=[bpQ], writes=[bQk[sl]])
                        yield

                def seq_chunk(c):
                    sl = c % 2
                    for h, p, qs in hloop():
                        hc = slice(h * 64, (h + 1) * 64)
                        S.mm(pS[qs, 0, p, :], ar[qs, p, c, 0, :], Hr[qs, p, :], start=True, stop=False, reads=[bar, bH], writes=[bpS])
                        S.mm(pS[qs, 0, p, :], A2m[sl][qs, p, 0:64], tms[2][qs, c, hc], start=False, stop=True, reads=[bA2m[sl], btm[2]], writes=[bpS])
                    yield
                    S.copy("act", Xs[:], pS[:, 0, :, :], reads=[bpS], writes=[bXs])
                    yield
                    for h, p, qs in hloop():
                        S.mm(pS[qs, 1, p, :], TT[sl][qs, p, :], Xs[qs, p, :], reads=[bTT[sl], bXs], writes=[bpS])
                    yield
                    S.copy("act", Us[:], pS[:, 1, :, :], reads=[bpS], writes=[bUs])
                    yield
                    for h, p, qs in hloop():
                        hc = slice(h * 64, (h + 1) * 64)
                        S.mm(pS[qs, 0, p, :], Hr[qs, p, :], ar[qs, p, c, 1, :], start=True, stop=False, reads=[bH, bar], writes=[bpS])
                        S.mm(pS[qs, 0, p, :], Us[qs, p, :], A1m[sl][qs, p, 64:128], start=False, stop=False, reads=[bUs, bA1m[sl]], writes=[bpS])
                        S.mm(pS[qs, 0, p, :], tms[2][qs, c, hc], A2m[sl][qs, p, 64:128], start=False, stop=True, reads=[btm[2], bA2m[sl]], writes=[bpS])
                    for h, p, qs in hloop():
                        hc = slice(h * 64, (h + 1) * 64)
                        S.mm(pS[qs, 1, p, :], tms[0][qs, c, hc], Us[qs, p, :], start=True, stop=False, reads=[btm[0], bUs], writes=[bpS])
                        S.mm(pS[qs, 1, p, :], tms[1][qs, c, hc], tms[2][qs, c, hc], start=False, stop=True, reads=[btm[1], btm[2]], writes=[bpS])
                    yield
                    S.copy("act", oT_[:, :, c * CH:(c + 1) * CH], pS[:, 0, :, :], reads=[bpS], writes=[boT_])
                    S.tt("dve", Hst[:], Hst[:], pS[:, 1, :, :], ALU.add, reads=[bH, bpS], writes=[bH])
                    S.tt("dve", Hst[:], Hst[:], gam[:, :, c:c + 1].to_broadcast([128, 4, 64]), ALU.mult, reads=[bH, bgam], writes=[bH])
                    if RDT != F32:
                        S.copy("pool", Hr[:], Hst[:], reads=[bH], writes=[bH])
                    yield

                srcs = [(btr, bbtr), (ktr, bktr), (vr, bvr)]
                ncp = 0
                for ai, (src, bsrc) in enumerate(srcs):
                    for c in range(NCB):
                        for p in range(4):
                            for hf in range(2):
                                S.mm(ptm[hf * 64:(hf + 1) * 64, p, :], src[:, p, c * CH:(c + 1) * CH], identr[:, :],
                                     reads=[bsrc, bcst], writes=[bptm])
                        S.copy("act" if ncp % 2 == 0 else "dve", tms[ai][:, c, :], ptm.rearrange("p a b -> p (a b)"),
                               reads=[bptm], writes=[btm[ai]])
                        ncp += 1
                        yield
                if i + 1 < NRB:
                    issue_loads(i + 1)

                g = prep_chunk(corder[0])
                for _ in g:
                    yield
                for ci, c in enumerate(corder):
                    gens = [seq_chunk(c)]
                    if ci + 1 < NCB:
                        gens.append(prep_chunk(corder[ci + 1]))
                    while gens:
                        for g in list(gens):
                            try:
                                next(g)
                            except StopIteration:
                                gens.remove(g)
                        yield
                S.dma("sp", orawd[d, :, :, t0:t0 + RB], oT_[:], boT_, reads=[boT_])
                yield

        threads = [dir_thread(d) for d in dirs]
        while threads:
            for g in list(threads):
                try:
                    next(g)
                except StopIteration:
                    threads.remove(g)
        S.end()

    def phase_rpost():
        RB = 512
        S.begin()
        par, bpar = rw_params([ln_g, ln_b])
        bcst = S.buf("const")
        blk2 = make_blk2(bcst)
        sh4 = [128, 4, RB]

        def pcol(c):
            return par[:, c:c + 4].unsqueeze(2).to_broadcast(sh4)

        oT = [S.sb(f"oT{i}", [128, 4, RB], F32) for i in range(2)]
        boT = S.bufs_n(2, "oT")
        bon = [S.sb(f"bon{i}", [128, 4, RB], F32) for i in range(2)]
        bbon = S.bufs_n(2, "bon")
        gt = [S.sb(f"gt{i}", [128, 4, RB], BF16) for i in range(2)]
        bgt = S.bufs_n(2, "gt")
        sq = S.sb("sq", [128, 4, RB], F32)
        mean = S.sb("mean", [128, 4, RB], F32)
        ex2 = S.sb("ex2", [128, 4, RB], F32)
        bsq, bmean, bex2 = S.buf("sq"), S.buf("mean"), S.buf("ex2")
        oo = [S.sb(f"oo{i}", [128, 4, RB], BF16) for i in range(2)]
        boo = S.bufs_n(2, "oo")
        pp = [S.ps(f"pp{i}", [128, 512], F32) for i in range(4)]
        bpp = S.bufs_n(4, "ps_pp")
        n = 0
        k = 0
        for blk in range(NB):
            t0 = blk * RB
            S.dma("sp", bon[blk % 2][:], bond[:, :, t0:t0 + RB], bbon[blk % 2], writes=[bbon[blk % 2]])
            for d in range(2):
                o_, bo_ = oT[k % 2], boT[k % 2]
                g_, bg_ = gt[k % 2], bgt[k % 2]
                q_, bq_ = oo[k % 2], boo[k % 2]
                k += 1
                S.dma("sp", o_[:], orawd[d, :, :, t0:t0 + RB], bo_, writes=[bo_])
                S.dma("sp", g_[:], gtd[d, :, :, t0:t0 + RB], bg_, writes=[bg_])
                S.act(sq[:], o_[:], AF.Square, reads=[bo_], writes=[bsq])
                for p in range(4):
                    p1, b1 = pp[n % 4], bpp[n % 4]
                    n += 1
                    S.mm(p1[:], blk2[:], o_[:, p, :], reads=[bcst, bo_], writes=[b1])
                    S.act(mean[:, p, :], p1[:], AF.Identity, reads=[b1], writes=[bmean], scale=1.0 / 64)
                    p2, b2 = pp[n % 4], bpp[n % 4]
                    n += 1
                    S.mm(p2[:], blk2[:], sq[:, p, :], reads=[bcst, bsq], writes=[b2])
                    S.act(ex2[:, p, :], p2[:], AF.Identity, reads=[b2], writes=[bex2], scale=1.0 / 64)
                S.act(sq[:], mean[:], AF.Square, reads=[bmean], writes=[bsq])
                S.tt("dve", ex2[:], ex2[:], sq[:], ALU.subtract, reads=[bex2, bsq], writes=[bex2])
                S.act(ex2[:], ex2[:], AF.Sqrt, reads=[bex2], writes=[bex2], bias=GN_EPS)
                S.recip(ex2[:], ex2[:], reads=[bex2], writes=[bex2])
                S.tt("dve", o_[:], o_[:], mean[:], ALU.subtract, reads=[bo_, bmean], writes=[bo_])
                S.tt("dve", o_[:], o_[:], ex2[:], ALU.mult, reads=[bo_, bex2], writes=[bo_])
                for p in range(4):
                    S.act(o_[:, p, :], o_[:, p, :], AF.Identity, reads=[bo_, bpar], writes=[bo_], scale=par[:, p:p + 1], bias=par[:, 4 + p:5 + p])
                S.tt("dve", o_[:], o_[:], bon[blk % 2][:], ALU.add, reads=[bo_, bbon[blk % 2]], writes=[bo_])
                S.tt("dve", q_[:], o_[:], g_[:], ALU.mult, reads=[bo_, bg_], writes=[bq_])
                dst = ofT_v if d == 0 else obT_v
                S.dma("sp", dst[:, :, t0:t0 + RB], q_[:], bq_, reads=[bq_])
        S.end()

    phase_rprep()
    if stop_after == "RP":
        gst.close()
        return nc
    phase_rscan()
    if stop_after == "RS":
        gst.close()
        return nc
    phase_rpost()
    if stop_after in ("R0", "R1"):
        gst.close()
        return nc

    def phase_merge():
        S.begin()
        wo_a = S.sb("wo_a", [64, 8, D], BF16)
        wo_b = S.sb("wo_b", [128, 4, D], BF16)
        wm = S.sb("wm", [128, 8, D], BF16)
        bwt = S.buf("wts")
        wst = [S.sb(f"wst{i}", [128, D], F32) for i in range(2)]
        bwst = S.bufs_n(2, "wst")
        n = 0
        for h in range(8):
            sl = n % 2
            S.dma("sp", wst[sl][0:64, :], w_o[h * 64:(h + 1) * 64, :], bwst[sl], writes=[bwst[sl]])
            S.copy(["dve", "pool"][n % 2], wo_a[:, h, :], wst[sl][0:64, :], reads=[bwst[sl]], writes=[bwt])
            n += 1
        for p in range(4):
            sl = n % 2
            S.dma("sp", wst[sl][:], w_o[512 + p * 128:512 + (p + 1) * 128, :], bwst[sl], writes=[bwst[sl]])
            S.copy(["dve", "pool"][n % 2], wo_b[:, p, :], wst[sl][:], reads=[bwst[sl]], writes=[bwt])
            n += 1
        for kc in range(8):
            sl = n % 2
            S.dma("sp", wst[sl][:], w_merge[kc * 128:(kc + 1) * 128, :], bwst[sl], writes=[bwst[sl]])
            S.copy(["dve", "pool"][n % 2], wm[:, kc, :], wst[sl][:], reads=[bwst[sl]], writes=[bwt])
            n += 1
        g2_bc = S.sb("g2_bc", [128, D], F32)
        bg = S.buf("g")
        S.dma("sp", g2_bc[:], norm_ffn_g.partition_broadcast(128), bg, writes=[bg])
        oa_sb = S.sb("oa_sb", [64, 8, 512], BF16)
        ob_sb = S.sb("ob_sb", [128, 4, 512], BF16)
        of_sb = S.sb("of_sb", [128, 4, 512], BF16)
        bof = S.buf("of_in")
        ofT_v = ofT.rearrange("a p s -> p a s")
        gt_sb = S.sb("gt_sb", [128, 16, 512], BF16)
        boa, bob, bgt = S.bufs_n(3, "in")
        z = S.sb("z", [128, 8, 512], BF16)
        bz = S.buf("z")
        t1 = [S.sb(f"t1_{i}", [128, 512], F32) for i in range(2)]
        t2 = [S.sb(f"t2_{i}", [128, 512], F32) for i in range(2)]
        bt1 = S.bufs_n(2, "t1")
        bt2 = S.bufs_n(2, "t2")
        pa = [S.ps(f"pa{i}", [128, 512], F32) for i in range(2)]
        pbk = [S.ps(f"pbk{i}", [128, 512], F32) for i in range(2)]
        bpa = S.bufs_n(2, "ps_a")
        bpbk = S.bufs_n(2, "ps_b")
        pmg = [S.ps(f"pmg{i}", [128, 512], F32) for i in range(2)]
        bpmg = S.bufs_n(2, "ps_mg")
        pT2 = S.ps("pT2", [128, 8, 128], BF16)
        bpT2 = S.buf("ps_T2")
        xt = [S.sb(f"xt{i}", [128, D], F32) for i in range(2)]
        bxt = S.bufs_n(2, "xt")
        x1t = [S.sb(f"x1t{i}", [128, D], F32) for i in range(2)]
        bx1t = S.bufs_n(2, "x1t")
        junk = S.sb("junk", [128, D], BF16)
        bjunk = S.buf("junk")
        st = S.sb("st", [128, 2], F32)
        bst = S.buf("st")
        h2n = S.sb("h2n", [128, D], BF16)
        bh2n = S.buf("h2n")
        h2t = [S.sb(f"h2t{i}", [128, 8, 128], BF16) for i in range(2)]
        bh2t = S.bufs_n(2, "h2t")
        oaT_v = oaT.rearrange("h d s -> d h s")
        obT_v = obT.rearrange("a p s -> p a s")
        gat_v = gat.rearrange("j p s -> p j s")
        h2T_v = h2T.rearrange("kc p s -> p kc s")
        nmg = 0
        for blk in range(NB):
            cs = slice(blk * 512, (blk + 1) * 512)
            S.dma("sp", oa_sb[:], oaT_v[:, :, cs], boa, writes=[boa])
            S.dma("sp", ob_sb[:], obT_v[:, :, cs], bob, writes=[bob])
            S.dma("sp", of_sb[:], ofT_v[:, :, cs], bof, writes=[bof])
            S.dma("sp", gt_sb[:, 0:8, :], gat_v[:, 0:8, cs], bgt, writes=[bgt])
            S.dma("sp", gt_sb[:, 8:16, :], gat_v[:, 8:16, cs], bgt, writes=[bgt])
            for fc in range(8):
                sl = fc % 2
                fs = slice(fc * 128, (fc + 1) * 128)
                for h in range(8):
                    S.mm(pa[sl][:], wo_a[:, h, fs], oa_sb[:, h, :], start=(h == 0), stop=(h == 7),
                         reads=[bwt, boa], writes=[bpa[sl]])
                for p in range(4):
                    S.mm(pbk[sl][:], wo_b[:, p, fs], of_sb[:, p, :], start=(p == 0), stop=False,
                         reads=[bwt, bof], writes=[bpbk[sl]])
                for p in range(4):
                    S.mm(pbk[sl][:], wo_b[:, p, fs], ob_sb[:, p, :], start=False, stop=(p == 3),
                         reads=[bwt, bob], writes=[bpbk[sl]])
                S.tt("dve", t1[sl][:], pa[sl][:], gt_sb[:, fc, :], ALU.mult, reads=[bpa[sl], bgt], writes=[bt1[sl]])
                S.tt("dve", t2[sl][:], pbk[sl][:], gt_sb[:, 8 + fc, :], ALU.mult, reads=[bpbk[sl], bgt], writes=[bt2[sl]])
                S.tt("pool", z[:, fc, :], t1[sl][:], t2[sl][:], ALU.add, reads=[bt1[sl], bt2[sl]], writes=[bz])
            for tl in range(4):
                t = blk * 4 + tl
                xb, bxb = xt[t % 2], bxt[t % 2]
                x1b, bx1b = x1t[t % 2], bx1t[t % 2]
                S.dma("sp", xb[:], x[t * 128:(t + 1) * 128, :], bxb, writes=[bxb])
                for hf in range(2):
                    pm_, bpm_ = pmg[nmg % 2], bpmg[nmg % 2]
                    nmg += 1
                    for kc in range(8):
                        S.mm(pm_[:], z[:, kc, tl * 128:(tl + 1) * 128], wm[:, kc, hf * 512:(hf + 1) * 512],
                             start=(kc == 0), stop=(kc == 7), reads=[bz, bwt], writes=[bpm_])
                    S.tt("dve", x1b[:, hf * 512:(hf + 1) * 512], pm_[:], xb[:, hf * 512:(hf + 1) * 512], ALU.add,
                         reads=[bpm_, bxb], writes=[bx1b])
                S.dma("sp", x1[t * 128:(t + 1) * 128, :], x1b[:], bx1b, reads=[bx1b])
                S.act(junk[:], x1b[:], AF.Square, reads=[bx1b], writes=[bjunk, bst], accum_out=st[:, 0:1])
                S.act(st[:, 0:1], st[:, 0:1], AF.Sqrt, reads=[bst], writes=[bst], scale=1.0 / D, bias=NORM_EPS)
                S.recip(st[:, 0:1], st[:, 0:1], reads=[bst], writes=[bst])
                S.stt(h2n[:], x1b[:], st[:, 0:1], g2_bc[:], ALU.mult, ALU.mult, reads=[bx1b, bst, bg], writes=[bh2n])
                for kc in range(8):
                    S.tr(pT2[:, kc, :], h2n[:, kc * 128:(kc + 1) * 128], identb[:], reads=[bh2n], writes=[bpT2])
                ho, bho = h2t[t % 2], bh2t[t % 2]
                S.copy("act", ho[:], pT2[:], reads=[bpT2], writes=[bho])
                S.dma("sp", h2T_v[:, :, t * 128:(t + 1) * 128], ho[:], bho, reads=[bho])
        S.end()

    phase_merge()
    if stop_after == "M":
        gst.close()
        return nc

    def phase_f1():
        S.begin()
        h2_sb = S.sb("h2_sb", [128, 8, S_LEN], BF16)
        bh2 = S.buf("h2")
        S.dma("sp", h2_sb[:], h2T.rearrange("kc p s -> p kc s"), bh2, writes=[bh2])
        cst = S.sb("cst", [4, FFN], F32)
        bcst = S.buf("cst")
        S.dma("sp", cst[0:3, :], conv_w[:, :], bcst, writes=[bcst])
        S.dma("sp", cst[3:4, :], conv_b.rearrange("(o n) -> o n", o=1), bcst, writes=[bcst])
        pcw = S.ps("pcw", [128, NHC, 4], F32)
        bpcw = S.buf("ps_cw")
        cw = S.sb("cw", [128, NHC, 4], F32)
        bcw = S.buf("cw")
        for hc in range(NHC):
            S.tr(pcw[:, hc, :], cst[0:4, hc * 128:(hc + 1) * 128], ident[0:4, 0:4], reads=[bcst], writes=[bpcw])
        S.copy("dve", cw[:], pcw[:], reads=[bpcw], writes=[bcw])
        wgst = [S.sb(f"wgst{i}", [128, 8, 128], F32) for i in range(2)]
        wust = [S.sb(f"wust{i}", [128, 8, 128], F32) for i in range(2)]
        wg = [S.sb(f"wg{i}", [128, 8, 128], BF16) for i in range(2)]
        wu = [S.sb(f"wu{i}", [128, 8, 128], BF16) for i in range(2)]
        bwgst, bwust, bwg, bwu = [S.bufs_n(2, nm) for nm in ("wgst", "wust", "wg", "wu")]
        G_sb = S.sb("G_sb", [128, S_LEN + 2], F32)
        bG = S.buf("G")
        S.memset("pool", G_sb[:, 0:1], 0.0, writes=[bG])
        S.memset("pool", G_sb[:, S_LEN + 1:S_LEN + 2], 0.0, writes=[bG])
        gp = S.sb("gp", [128, S_LEN], F32)
        bgp = S.buf("gp")
        sg = S.sb("sg", [128, S_LEN], F32)
        bsg = S.buf("sg")
        at_sb = [S.sb(f"at{i}", [128, 512], BF16) for i in range(3)]
        bat = S.bufs_n(3, "at")
        pg = [S.ps(f"pg{i}", [128, 512], F32) for i in range(2)]
        bpg = S.bufs_n(2, "ps_g")
        pu = [S.ps(f"pu{i}", [128, 512], F32) for i in range(2)]
        bpu = S.bufs_n(2, "ps_u")
        wg_v = w_gate.rearrange("(kc p) n -> p kc n", p=128)
        wu_v = w_up.rearrange("(kc p) n -> p kc n", p=128)
        nat = [0]
        gp2 = [gp, S.sb("gp_b", [128, S_LEN], F32)]
        sg2 = [sg, S.sb("sg_b", [128, S_LEN], F32)]
        bgp2 = [bgp, S.buf("gp_b")]
        bsg2 = [bsg, S.buf("sg_b")]

        def gate_part(hc):
            sl = hc % 2
            hs = slice(hc * 128, (hc + 1) * 128)
            gp_, bgp_, sg_, bsg_ = gp2[sl], bgp2[sl], sg2[sl], bsg2[sl]
            S.dma("sp", wgst[sl][:], wg_v[:, :, hs], bwgst[sl], writes=[bwgst[sl]])
            S.dma("sp", wust[sl][:], wu_v[:, :, hs], bwust[sl], writes=[bwust[sl]])
            S.copy("pool", wg[sl][:], wgst[sl][:], reads=[bwgst[sl]], writes=[bwg[sl]])
            S.copy("pool", wu[sl][:], wust[sl][:], reads=[bwust[sl]], writes=[bwu[sl]])
            for tb in range(NB):
                p_, bp_ = pg[tb % 2], bpg[tb % 2]
                for kc in range(8):
                    S.mm(p_[:], wg[sl][:, kc, :], h2_sb[:, kc, tb * 512:(tb + 1) * 512], start=(kc == 0), stop=(kc == 7),
                         reads=[bwg[sl], bh2], writes=[bp_])
                S.copy("act", G_sb[:, 1 + tb * 512:1 + (tb + 1) * 512], p_[:], reads=[bp_], writes=[bG])
            S.ts("pool", gp_[:], G_sb[:, 1:S_LEN + 1], cw[:, hc, 1:2], cw[:, hc, 3:4], ALU.mult, ALU.add,
                 reads=[bG, bcw], writes=[bgp_])
            S.stt(gp_[:], G_sb[:, 0:S_LEN], cw[:, hc, 0:1], gp_[:], ALU.mult, ALU.add, reads=[bG, bcw, bgp_], writes=[bgp_])
            S.stt(gp_[:], G_sb[:, 2:S_LEN + 2], cw[:, hc, 2:3], gp_[:], ALU.mult, ALU.add, reads=[bG, bcw, bgp_], writes=[bgp_])
            S.act(sg_[:], gp_[:], AF.Silu, reads=[bgp_], writes=[bsg_])

        def up_part(hc):
            sl = hc % 2
            sg_, bsg_ = sg2[sl], bsg2[sl]
            for tb in range(NB):
                p_, bp_ = pu[tb % 2], bpu[tb % 2]
                for kc in range(8):
                    S.mm(p_[:], wu[sl][:, kc, :], h2_sb[:, kc, tb * 512:(tb + 1) * 512], start=(kc == 0), stop=(kc == 7),
                         reads=[bwu[sl], bh2], writes=[bp_])
                a_, ba_ = at_sb[nat[0] % 3], bat[nat[0] % 3]
                nat[0] += 1
                S.tt("dve", a_[:], p_[:], sg_[:, tb * 512:(tb + 1) * 512], ALU.mult, reads=[bp_, bsg_], writes=[ba_])
                S.dma("sp", actT[tb * 4:(tb + 1) * 4, :, hc, :].rearrange("t p s -> p t s"),
                      a_[:].rearrange("p (t s) -> p t s", s=128), ba_, reads=[ba_])

        for hc in range(NHC):
            gate_part(hc)
            if hc >= 1:
                up_part(hc - 1)
        up_part(NHC - 1)
        S.end()

    phase_f1()
    if stop_after == "F1":
        gst.close()
        return nc

    def phase_f2():
        S.begin()
        wd = S.sb("wd", [128, NHC, D], BF16)
        bwd = S.buf("wd")
        wst = [S.sb(f"wst{i}", [128, D], F32) for i in range(2)]
        bwst = S.bufs_n(2, "wst")
        for hc in range(NHC):
            sl = hc % 2
            S.dma("sp", wst[sl][:], w_down[hc * 128:(hc + 1) * 128, :], bwst[sl], writes=[bwst[sl]])
            S.copy(["dve", "pool"][hc % 2], wd[:, hc, :], wst[sl][:], reads=[bwst[sl]], writes=[bwd])
        at = [S.sb(f"at{i}", [128, NHC, 128], BF16) for i in range(2)]
        bat = S.bufs_n(2, "at")
        x1t = [S.sb(f"x1t{i}", [128, D], F32) for i in range(2)]
        bx1t = S.bufs_n(2, "x1t")
        yt = [S.sb(f"yt{i}", [128, D], F32) for i in range(2)]
        byt = S.bufs_n(2, "yt")
        pd = [S.ps(f"pd{i}", [128, 512], F32) for i in range(2)]
        bpd = S.bufs_n(2, "ps_d")
        n = 0
        for t in range(NT):
            a_, ba_ = at[t % 2], bat[t % 2]
            xb, bxb = x1t[t % 2], bx1t[t % 2]
            yb, byb = yt[t % 2], byt[t % 2]
            S.dma("sp", a_[:], actT[t, :, :, :], ba_, writes=[ba_])
            S.dma("sp", xb[:], x1[t * 128:(t + 1) * 128, :], bxb, writes=[bxb])
            for hf in range(2):
                p_, bp_ = pd[n % 2], bpd[n % 2]
                n += 1
                for hc in range(NHC):
                    S.mm(p_[:], a_[:, hc, :], wd[:, hc, hf * 512:(hf + 1) * 512], start=(hc == 0), stop=(hc == NHC - 1),
                         reads=[ba_, bwd], writes=[bp_])
                S.tt("dve", yb[:, hf * 512:(hf + 1) * 512], p_[:], xb[:, hf * 512:(hf + 1) * 512], ALU.add,
                     reads=[bp_, bxb], writes=[byb])
            S.dma("sp", y[t * 128:(t + 1) * 128, :], yb[:], byb, reads=[byb])
        S.end()

    phase_f2()

    gst.close()
    return nc


INPUT_NAMES = ["x", "positions", "norm_mix_g", "w_in", "b_gate", "q_a_norm_g", "kv_a_norm_g",
               "w_uq", "w_ukv", "qn_norm_g", "qr_norm_g", "kn_norm_g", "kr_norm_g", "shift_mu",
               "w0", "w2", "a0", "a2", "g2", "k_k", "k_a", "r_k", "ln_x_g", "ln_x_b", "w_o",
               "w_merge", "norm_ffn_g", "w_ffn_gate", "w_ffn_up", "ffn_conv_w", "ffn_conv_b",
               "w_ffn_down"]


def make_in_maps(inputs, n_cores, S_LEN):
    inv_freq = (10000.0 ** (-np.arange(0, ROPE, 2, dtype=np.float32) / ROPE)).astype(np.float32)
    maps = []
    for c in range(n_cores):
        m = {"inv_freq": inv_freq}
        for k in INPUT_NAMES:
            a = np.asarray(inputs[k])
            if k == "x":
                m[k] = np.ascontiguousarray(a[c, :S_LEN])
            elif k == "positions":
                m[k] = np.ascontiguousarray(a[c, :S_LEN]).astype(np.int32)
            else:
                a = a[0]
                if k == "r_k":
                    a = a.reshape(512)
                m[k] = np.ascontiguousarray(a).astype(np.float32)
        maps.append(m)
    return maps


_NC_CACHE = {}


def kernel(**inputs):
    S_LEN = 4096
    B = 8
    if "nc" not in _NC_CACHE:
        _NC_CACHE["nc"] = build(S_LEN)
    nc = _NC_CACHE["nc"]
    maps = make_in_maps(inputs, B, S_LEN)
    res = run_bass_kernel_spmd(nc, maps, core_ids=list(range(B)))
    out = np.stack([np.asarray(r["y"]) for r in res.results], axis=0)
    return out.astype(np.float32)
```
